# Optimizing a Trainium2 kernel written in Bass

```python
import math
import jax
import jax.numpy as jnp
from jax import lax
import numpy as np

D_MODEL = 1024
BATCH = 8
SEQ = 4096
DEPTH = 2

CHUNK = 64
D_MIX = D_MODEL
EPS = 1e-6

GM_WIDTH = D_MIX // 4
GM_HEADS = 4
GM_HEAD_DIM = GM_WIDTH // GM_HEADS
GM_BLOCK = 128

CV_WIDTH = D_MIX // 4
CV_KERNEL = 31

DN_WIDTH = D_MIX // 2
DN_HEAD_DIM = 128
DN_HEADS = DN_WIDTH // DN_HEAD_DIM
DN_CONV = 4

D_FF = -(-8 * D_MODEL // (3 * 256)) * 256

SPLIT_SIZES = (GM_WIDTH, GM_WIDTH, CV_WIDTH, CV_WIDTH,
               DN_WIDTH, DN_WIDTH, DN_WIDTH, DN_WIDTH, DN_HEADS, DN_HEADS)
D_PROJ = 2 * GM_WIDTH + 2 * CV_WIDTH + 4 * DN_WIDTH + 2 * DN_HEADS

kernel_name = "hybrid_gmlp_conv_deltanet_block"


def _split_offsets(sizes):
    out, s = [], 0
    for n in sizes[:-1]:
        s += n
        out.append(s)
    return out


def rms_norm(x, g):
    xf = x.astype(jnp.float32)
    y = xf * lax.rsqrt(jnp.mean(xf * xf, axis=-1, keepdims=True) + EPS)
    return (y * g.astype(jnp.float32)).astype(x.dtype)


def layer_norm(x, g, b):
    xf = x.astype(jnp.float32)
    mu = jnp.mean(xf, axis=-1, keepdims=True)
    var = jnp.mean(jnp.square(xf - mu), axis=-1, keepdims=True)
    y = (xf - mu) * lax.rsqrt(var + EPS)
    return (y * g.astype(jnp.float32) + b.astype(jnp.float32)).astype(x.dtype)


def causal_depthwise_conv(x, w):
    k = w.shape[0]
    return lax.conv_general_dilated(
        x, w[:, None, :].astype(x.dtype), window_strides=(1,), padding=[(k - 1, 0)],
        dimension_numbers=("NWC", "WIO", "NWC"), feature_group_count=x.shape[-1])


def spatial_gating(u, v, ln_g, ln_b, ws, bs):
    b, t, _ = v.shape
    v = layer_norm(v, ln_g, ln_b)
    causal = jnp.tril(jnp.ones((GM_BLOCK, GM_BLOCK), dtype=bool))
    ws = jnp.where(causal, ws, jnp.zeros_like(ws))
    vb = v.reshape(b, t // GM_BLOCK, GM_BLOCK, GM_HEADS, GM_HEAD_DIM)
    s = jnp.einsum("hij,bnjhd->bnihd", ws, vb) + bs.T[:, :, None]
    return u * s.reshape(b, t, GM_WIDTH)


def conformer_conv(a, gate, dw_w, dw_b, ln_g, ln_b):
    y = a * jax.nn.sigmoid(gate)
    y = causal_depthwise_conv(y, dw_w) + dw_b
    y = layer_norm(y, ln_g, ln_b)
    return jax.nn.silu(y)


def l2_normalize(x):
    return x * lax.rsqrt(jnp.sum(x * x, axis=-1, keepdims=True) + EPS)


def chunked_gated_delta_rule(q, k, v, beta, log_alpha):
    b, t, h, dk = q.shape
    dv = v.shape[-1]
    n = t // CHUNK

    def chunks(z):
        return z.reshape(b, n, CHUNK, h, -1).transpose(0, 3, 1, 2, 4)

    def chunks_scalar(z):
        return z.reshape(b, n, CHUNK, h).transpose(0, 3, 1, 2)

    qc, kc, vc = chunks(q), chunks(k), chunks(v)
    bc = chunks_scalar(beta)
    gam = jnp.cumsum(chunks_scalar(log_alpha), axis=-1)

    diff = gam[..., :, None] - gam[..., None, :]
    strict = jnp.tril(jnp.ones((CHUNK, CHUNK), dtype=bool), -1)
    incl = jnp.tril(jnp.ones((CHUNK, CHUNK), dtype=bool))
    decay_strict = jnp.exp(jnp.where(strict, diff, -jnp.inf))
    decay_incl = jnp.exp(jnp.where(incl, diff, -jnp.inf))

    kk = jnp.einsum("bhnid,bhnjd->bhnij", kc, kc)
    lower = jnp.eye(CHUNK, dtype=jnp.float32) + bc[..., :, None] * kk * decay_strict
    rhs = jnp.concatenate([bc[..., None] * vc, (bc * jnp.exp(gam))[..., None] * kc], axis=-1)
    sol = lax.linalg.triangular_solve(lower, rhs, left_side=True, lower=True)
    u_part, w_part = sol[..., :dv], sol[..., dv:]

    qk = jnp.einsum("bhnid,bhnjd->bhnij", qc, kc) * decay_incl
    q_dec = qc * jnp.exp(gam)[..., None]
    k_dec = kc * jnp.exp(gam[..., -1:] - gam)[..., None]
    chunk_decay = jnp.exp(gam[..., -1])

    def step(state, inp):
        u_n, w_n, qk_n, qd_n, kd_n, cd_n = inp
        u = u_n - jnp.einsum("bhck,bhkv->bhcv", w_n, state)
        o = jnp.einsum("bhck,bhkv->bhcv", qd_n, state) + jnp.einsum("bhij,bhjv->bhiv", qk_n, u)
        state = cd_n[..., None, None] * state + jnp.einsum("bhck,bhcv->bhkv", kd_n, u)
        return state, o

    xs = tuple(jnp.moveaxis(z, 2, 0) for z in (u_part, w_part, qk, q_dec, k_dec, chunk_decay))
    s0 = jnp.zeros((b, h, dk, dv), jnp.float32)
    _, o = lax.scan(step, s0, xs)
    return o.transpose(1, 0, 3, 2, 4).reshape(b, t, h, dv)


def gated_deltanet(q, k, v, gate, beta_logit, alpha_logit, conv_w, a_log, dt_bias, norm_g):
    b, t, _ = q.shape
    dtype = q.dtype
    qkv = jax.nn.silu(causal_depthwise_conv(jnp.concatenate([q, k, v], axis=-1), conv_w))
    q, k, v = jnp.split(qkv.astype(jnp.float32), 3, axis=-1)
    q = l2_normalize(q.reshape(b, t, DN_HEADS, DN_HEAD_DIM)) * (DN_HEAD_DIM ** -0.5)
    k = l2_normalize(k.reshape(b, t, DN_HEADS, DN_HEAD_DIM))
    v = v.reshape(b, t, DN_HEADS, DN_HEAD_DIM)
    beta = jax.nn.sigmoid(beta_logit.astype(jnp.float32))
    log_alpha = -jnp.exp(a_log.astype(jnp.float32)) * jax.nn.softplus(
        alpha_logit.astype(jnp.float32) + dt_bias.astype(jnp.float32))
    o = chunked_gated_delta_rule(q, k, v, beta, log_alpha)
    g = gate.astype(jnp.float32).reshape(b, t, DN_HEADS, DN_HEAD_DIM)
    o = rms_norm(o, norm_g) * jax.nn.silu(g)
    return o.reshape(b, t, DN_WIDTH).astype(dtype)


def setup_inputs(seed: int = 0) -> dict:
    key = jax.random.key(seed)
    ks = jax.random.split(key, 22)
    f32 = jnp.float32
    L = DEPTH

    def normal(k, shape, scale):
        return scale * jax.random.normal(k, shape, f32)

    def gain(k, shape):
        return 1.0 + 0.05 * jax.random.normal(k, shape, f32)

    dt = jnp.exp(jax.random.uniform(ks[13], (L, DN_HEADS), f32, math.log(1e-3), math.log(1e-1)))
    return {
        "x": normal(ks[0], (BATCH, SEQ, D_MODEL), 1.0),
        "mix_norm_g": gain(ks[1], (L, D_MODEL)),
        "w_in": normal(ks[2], (L, D_MODEL, D_PROJ), D_MODEL ** -0.5),
        "gm_ln_g": gain(ks[3], (L, GM_WIDTH)),
        "gm_ln_b": normal(ks[4], (L, GM_WIDTH), 0.02),
        "gm_ws": normal(ks[5], (L, GM_HEADS, GM_BLOCK, GM_BLOCK), GM_BLOCK ** -0.5),
        "gm_bs": 1.0 + normal(ks[6], (L, GM_HEADS, GM_BLOCK), 0.02),
        "cv_dw_w": normal(ks[7], (L, CV_KERNEL, CV_WIDTH), CV_KERNEL ** -0.5),
        "cv_dw_b": normal(ks[8], (L, CV_WIDTH), 0.02),
        "cv_ln_g": gain(ks[9], (L, CV_WIDTH)),
        "cv_ln_b": normal(ks[10], (L, CV_WIDTH), 0.02),
        "dn_conv_w": normal(ks[11], (L, DN_CONV, 3 * DN_WIDTH), DN_CONV ** -0.5),
        "dn_a_log": jnp.log(jax.random.uniform(ks[12], (L, DN_HEADS), f32, 1.0, 16.0)),
        "dn_dt_bias": dt + jnp.log(-jnp.expm1(-dt)),
        "dn_norm_g": gain(ks[14], (L, DN_HEAD_DIM)),
        "w_out": normal(ks[15], (L, D_MIX, D_MODEL), D_MIX ** -0.5),
        "ffn_norm_g": gain(ks[16], (L, D_MODEL)),
        "w_gate": normal(ks[17], (L, D_MODEL, D_FF), D_MODEL ** -0.5),
        "w_up": normal(ks[18], (L, D_MODEL, D_FF), D_MODEL ** -0.5),
        "w_down": normal(ks[19], (L, D_FF, D_MODEL), D_FF ** -0.5),
        "final_norm_g": gain(ks[20], (D_MODEL,)),
    }


def reference(x, mix_norm_g, w_in, gm_ln_g, gm_ln_b, gm_ws, gm_bs, cv_dw_w, cv_dw_b,
              cv_ln_g, cv_ln_b, dn_conv_w, dn_a_log, dn_dt_bias, dn_norm_g, w_out,
              ffn_norm_g, w_gate, w_up, w_down, final_norm_g):
    offsets = _split_offsets(SPLIT_SIZES)
    for l in range(DEPTH):
        h = rms_norm(x, mix_norm_g[l])
        p = jnp.einsum("btd,de->bte", h, w_in[l])
        (gm_u, gm_v, cv_a, cv_g, dn_q, dn_k, dn_v, dn_g,
         dn_beta, dn_alpha) = jnp.split(p, offsets, axis=-1)
        y_a = spatial_gating(jax.nn.gelu(gm_u, approximate=False), jax.nn.gelu(gm_v, approximate=False),
                             gm_ln_g[l], gm_ln_b[l], gm_ws[l], gm_bs[l])
        y_b = conformer_conv(cv_a, cv_g, cv_dw_w[l], cv_dw_b[l], cv_ln_g[l], cv_ln_b[l])
        y_c = gated_deltanet(dn_q, dn_k, dn_v, dn_g, dn_beta, dn_alpha, dn_conv_w[l],
                             dn_a_log[l], dn_dt_bias[l], dn_norm_g[l])
        mix = jnp.concatenate([y_a, y_b, y_c], axis=-1)
        x = x + jnp.einsum("bte,ed->btd", mix, w_out[l])
        h = rms_norm(x, ffn_norm_g[l])
        ff = jax.nn.silu(jnp.einsum("btd,df->btf", h, w_gate[l])) * jnp.einsum("btd,df->btf", h, w_up[l])
        x = x + jnp.einsum("btf,fd->btd", ff, w_down[l])
    return rms_norm(x, final_norm_g)
```

```python
import contextlib
import numpy as np
import concourse.bass as bass
import concourse.mybir as mybir
from concourse.bass_utils import run_bass_kernel_spmd

F32 = mybir.dt.float32
BF16 = mybir.dt.bfloat16
AF = mybir.ActivationFunctionType
ALU = mybir.AluOpType

ENGS = ["tensor", "vector", "scalar", "gpsimd", "sync"]
EPOCH = 1500
_ESZ = {F32: 4, BF16: 2}


class _Op:
    __slots__ = ("eng", "fn", "deps", "needed", "sem", "val", "dma", "idx")

    def __init__(self, eng, fn, dma):
        self.eng = eng
        self.fn = fn
        self.deps = []
        self.needed = False
        self.sem = None
        self.val = 0
        self.dma = dma


class _Rec:
    __slots__ = ("plo", "phi", "iv", "lo", "hi", "writer", "readers", "dreaders")

    def __init__(self, reg, writer):
        _, self.plo, self.phi, self.iv = reg
        self.lo = self.iv[0][0]
        self.hi = max(b for _, b in self.iv)
        self.writer = writer
        self.readers = {}
        self.dreaders = []


def _expand(dims, base, esz):
    dims = [(s, c) for s, c in dims if c > 1 and s != 0]
    if not dims:
        return [(base * esz, (base + 1) * esz)]
    dims.sort(key=lambda d: -abs(d[0]))
    starts = [base]
    i = 0
    while i < len(dims):
        s, c = dims[i]
        inner = 1 + sum(abs(ss) * (cc - 1) for ss, cc in dims[i + 1:])
        if s >= inner and len(starts) * c <= 64 and i < len(dims) - 1:
            starts = [st + s * j for st in starts for j in range(c)]
            i += 1
        else:
            break
    ext = 1 + sum(abs(ss) * (cc - 1) for ss, cc in dims[i:])
    return [(st * esz, (st + ext) * esz) for st in starts]


class Sched:
    def __init__(self, nc):
        self.nc = nc
        self.ops = []
        self.by_eng = {e: [] for e in ENGS}
        self.hist = {}
        self.groups = {}
        self._psreaders = set()
        self.unordered = set()

    def _region(self, a):
        if isinstance(a, str):
            return (a, 0, 1, [(0, 1)])
        if str(a.space) == "PSUM":
            return (a.tensor.name, 0, 128, [(0, 2048)])
        pat = a.ap
        off = a.offset
        esz = _ESZ[a.dtype]
        row = pat[0][0]
        if row == 0:
            row = 1 << 40
        plo = off // row
        phi = plo + pat[0][1]
        f0 = off % row
        iv = _expand(list(pat[1:]), f0, esz)
        return (a.tensor.name, plo, phi, iv)

    @staticmethod
    def _overlap(rec, reg):
        _, plo, phi, iv = reg
        if plo >= rec.phi or phi <= rec.plo:
            return False
        for lo, hi in iv:
            if lo >= rec.hi or hi <= rec.lo:
                continue
            for rlo, rhi in rec.iv:
                if lo < rhi and rlo < hi:
                    return True
        return False

    @staticmethod
    def _covers(reg, rec):
        _, plo, phi, iv = reg
        if plo > rec.plo or phi < rec.phi:
            return False
        for rlo, rhi in rec.iv:
            ok = False
            for lo, hi in iv:
                if lo <= rlo and hi >= rhi:
                    ok = True
                    break
            if not ok:
                return False
        return True

    def add(self, eng, fn, outs=(), ins=(), dma=None):
        op = _Op(eng, fn, dma)
        deps = {}
        outs = [a for a in outs if a is not None]
        psr = []
        for a in ins:
            if a is None or isinstance(a, (int, float)):
                continue
            if not isinstance(a, str) and str(a.space) == "PSUM":
                psr.append(a)
                continue
            reg = self._region(a)
            for rec in self.hist.get(reg[0], ()):
                if self._overlap(rec, reg):
                    if rec.writer is not None:
                        deps[rec.writer] = "raw"
                    if dma is not None:
                        rec.dreaders.append(op)
                    else:
                        rec.readers[eng] = op
        for a in list(outs) + psr:
            reg = self._region(a)
            is_psr = any(a is b for b in psr)
            recs = self.hist.get(reg[0], [])
            keep = []
            for rec in recs:
                if self._overlap(rec, reg):
                    if rec.writer is not None and rec.writer is not op:
                        if not (is_psr and rec.writer.eng == eng and rec.writer in self._psreaders):
                            deps.setdefault(rec.writer, "raw" if is_psr else "waw")
                    for r in list(rec.readers.values()) + rec.dreaders:
                        if r is not op:
                            deps.setdefault(r, "war")
                    if self._covers(reg, rec):
                        continue
                keep.append(rec)
            keep.append(_Rec(reg, op))
            self.hist[reg[0]] = keep
            if is_psr and not any(str(getattr(b, "space", "")) == "PSUM" for b in outs):
                self._psreaders.add(op)
        for d, kind in deps.items():
            if d.dma is None and dma is None and d.eng == eng and eng == "tensor":
                continue
            op.deps.append(d)
            d.needed = True
        self.ops.append(op)
        self.by_eng[eng].append(op)
        if dma is not None:
            grp = self.groups.setdefault(dma, [])
            if grp and dma not in self.unordered and grp[-1] not in op.deps:
                op.deps.append(grp[-1])
            grp.append(op)
        return op

    def mm(self, out, lhsT, rhs, start=True, stop=True):
        return self.add("tensor", lambda e: e.matmul(out, lhsT=lhsT, rhs=rhs, start=start, stop=stop),
                        outs=[out], ins=[lhsT, rhs])

    def tr(self, out, in_, ident):
        return self.add("tensor", lambda e: e.transpose(out, in_, ident), outs=[out], ins=[in_, ident])

    def act(self, out, in_, func, bias=None, scale=None, accum_out=None, eng="scalar"):
        kw = {}
        if bias is not None:
            kw["bias"] = bias
        if scale is not None:
            kw["scale"] = scale
        if accum_out is not None:
            kw["accum_out"] = accum_out
        ins = [in_] + [v for v in (bias, scale) if not isinstance(v, (int, float)) and v is not None]
        return self.add(eng, lambda e: e.activation(out=out, in_=in_, func=func, **kw),
                        outs=[out, accum_out], ins=ins)

    def tt(self, eng, out, in0, in1, op):
        return self.add(eng, lambda e: e.tensor_tensor(out=out, in0=in0, in1=in1, op=op),
                        outs=[out], ins=[in0, in1])

    def ts(self, eng, out, in0, s1, s2=None, op0=ALU.mult, op1=None):
        kw = {}
        if op1 is not None:
            kw["op1"] = op1
        return self.add(eng, lambda e: e.tensor_scalar(out=out, in0=in0, scalar1=s1, scalar2=s2, op0=op0, **kw),
                        outs=[out], ins=[in0, s1, s2])

    def stt(self, eng, out, in0, scalar, in1, op0, op1):
        return self.add(eng, lambda e: e.scalar_tensor_tensor(out=out, in0=in0, scalar=scalar, in1=in1,
                                                              op0=op0, op1=op1),
                        outs=[out], ins=[in0, scalar, in1])

    def copy(self, eng, out, in_):
        if eng == "scalar":
            return self.add(eng, lambda e: e.activation(out=out, in_=in_, func=AF.Copy), outs=[out], ins=[in_])
        return self.add(eng, lambda e: e.tensor_copy(out=out, in_=in_), outs=[out], ins=[in_])

    def recip(self, out, in_):
        return self.add("vector", lambda e: e.reciprocal(out=out, in_=in_), outs=[out], ins=[in_])

    def memset(self, eng, out, val):
        return self.add(eng, lambda e: e.memset(out, val), outs=[out], ins=[])

    def dma(self, eng, out, in_, key, outs=None, ins=None):
        return self.add(eng, lambda e: e.dma_start(out=out, in_=in_),
                        outs=outs if outs is not None else [out],
                        ins=ins if ins is not None else [in_], dma=key)

    def finalize(self, stack, group_keys=()):
        nc = self.nc
        nsem = [0]

        def newsem():
            nsem[0] += 1
            return stack.enter_context(nc.semaphore("s%d" % nsem[0]))

        dsem = {}
        dcnt = {}
        for key, ops in self.groups.items():
            dsem[key] = newsem()
            dcnt[key] = 0
        for op in self.ops:
            if op.dma is not None:
                dcnt[op.dma] += 16
                op.sem = dsem[op.dma]
                op.val = dcnt[op.dma]
        for key in group_keys:
            for op in self.groups.get(key, ()):
                op.val = dcnt[key]
        for eng in ENGS:
            sem = None
            cnt = 0
            for op in self.by_eng[eng]:
                if op.dma is not None or not op.needed:
                    continue
                if sem is None or cnt >= EPOCH:
                    sem = newsem()
                    cnt = 0
                cnt += 1
                op.sem = sem
                op.val = cnt
        self.nsem = nsem[0]
        block = stack.enter_context(nc.Block())
        for eng in ENGS:
            ops = self.by_eng[eng]
            if not ops:
                continue

            def body(e, ops=ops):
                waited = {}
                for op in ops:
                    for d in op.deps:
                        k = id(d.sem)
                        if waited.get(k, 0) < d.val:
                            e.wait_ge(d.sem, d.val)
                            waited[k] = d.val
                    ins = op.fn(e)
                    if ins is not None and op.sem is not None:
                        ins.then_inc(op.sem, 16 if op.dma is not None else 1)

            getattr(block, eng)(body)


D = 1024
DPROJ = 3080
DFF = 2816
L = 2
TT = 512
NG = 4
EPS = 1e-6
NSLOT = 4
NP = 909
C_MIXG, C_FFNG, C_CVW, C_CVB, C_CVLG, C_CVLB, C_DNW, C_DNG, C_ALOG, C_DTB, C_BS, C_GLG, C_GLB = (
    0, 8, 16, 78, 80, 82, 84, 132, 133, 137, 141, 397, 653)
K_ID, K_MINCL, K_MNEG, K_MSTR, K_LEV1, K_ONES, K_LEVT = 0, 128, 256, 384, 512, 640, 768
NCST = 768 + 6 * 128


def _slab_table():
    tab = []
    for j in range(12):
        tab.append(("w_in", "col", j * 256, (j + 1) * 256))
    for j in range(4):
        tab.append(("w_out", "col", j * 256, (j + 1) * 256))
    for sc in range(6):
        nfc = 4 if sc < 5 else 2
        for hh in range(nfc // 2):
            c0 = sc * 512 + hh * 256
            tab.append(("w_gate", "col", c0, c0 + 256))
            tab.append(("w_up", "col", c0, c0 + 256))
        for hh in range(nfc // 2):
            k0 = sc * 4 + hh * 2
            tab.append(("w_down", "row", k0, k0 + 2))
    return tab


MR = [1, 1]


def build(n_tiles=8, n_layers=2, dumps=(), stop=None, pipeline=True):
    nc = bass.Bass("TRN2", target_bir_lowering=False)
    NTOK = n_tiles * TT
    x_d = nc.dram_tensor("x", [NTOK, D], F32, kind="ExternalInput").ap()
    wsrc = {
        "w_in": nc.dram_tensor("w_in", [L, D, DPROJ], F32, kind="ExternalInput").ap(),
        "w_out": nc.dram_tensor("w_out", [L, D, D], F32, kind="ExternalInput").ap(),
        "w_gate": nc.dram_tensor("w_gate", [L, D, DFF], F32, kind="ExternalInput").ap(),
        "w_up": nc.dram_tensor("w_up", [L, D, DFF], F32, kind="ExternalInput").ap(),
        "w_down": nc.dram_tensor("w_down", [L, DFF, D], F32, kind="ExternalInput").ap(),
    }
    prm_d = nc.dram_tensor("params", [128, L, NP], F32, kind="ExternalInput").ap()
    wsT_d = nc.dram_tensor("wsT", [128, L, 512], F32, kind="ExternalInput").ap()
    fng_d = nc.dram_tensor("fng", [128, D], F32, kind="ExternalInput").ap()
    cst_d = nc.dram_tensor("consts", [128, NCST], F32, kind="ExternalInput").ap()
    out_d = nc.dram_tensor("out", [NTOK, D], F32, kind="ExternalOutput").ap()
    tab = _slab_table()
    NSL = len(tab)
    wq = nc.dram_tensor("wq", [L * NSL, 128, 2048], BF16, kind="Internal").ap()
    dump_d = {}

    S = Sched(nc)
    PREK = tuple("pre%d%s" % (l, ab) for l in range(L) for ab in "ab")
    S.unordered = set(("c", "cw") + PREK)
    A = nc.alloc_sbuf_tensor
    P = nc.alloc_psum_tensor

    xres2 = A("xres", [128, 2, NG, D], F32)
    actT2 = A("actT", [128, 2, 8, TT], BF16)
    hb = A("hb", [128, D], BF16)
    ringF = A("ringF", [128, 4, 2048], BF16)
    ringP = A("ringP", [128, 2, 2048], BF16)
    cst = A("cst", [128, NCST], F32)
    prm = A("prm", [128, L, NP], F32)
    ident_b = A("ident_b", [128, 128], BF16)
    ones_b = A("ones_b", [128, 128], BF16)
    lev1n = A("lev1n", [128, 128], F32)
    wsTm = A("wsTm", [128, L, 4, 128], BF16)
    wba = A("wba", [128, L, 8, 8], BF16)
    negA = A("negA", [128, L, 4], F32)
    ss = A("ss", [128, 4], F32)
    sd = A("sd", [128, 4], F32)
    rstd = A("rstd", [128, 4], F32)
    ugu = A("ugu", [128, NG, 256], BF16)
    vg = A("vg", [128, NG, 256], F32)
    vn = A("vn", [128, NG, 256], BF16)
    st6 = A("st6", [128, NG, 6], F32)
    mv = A("mv", [128, NG, 2], F32)
    sdA = A("sdA", [128, 4], F32)
    rsA = A("rsA", [128, 4], F32)
    vh = A("vh", [128, 2, 256], F32)
    tA = A("tA", [128, 2, 256], F32)
    ya = A("ya", [128, 2, 256], BF16)
    ypre = A("ypre", [128, 2, 30 + TT], BF16)
    halo_cv = A("halo_cv", [128, L, 2, 30], BF16)
    sig = A("sig", [128, 2, TT], F32)
    sg = A("sg", [128, 1, TT], F32)
    cacc = A("cacc", [128, 2, TT], F32)
    fgb = cacc[:].rearrange("p a b -> p (a b)")
    tB = A("tB", [128, 2, TT], F32)
    csq = tB
    meanB = A("meanB", [128, TT], F32)
    varB = A("varB", [128, TT], F32)
    dg = A("dg", [128, 8, 128], BF16)
    qkvpre = A("qkvpre", [128, 4, 3 + TT], BF16)
    halo_dn = A("halo_dn", [128, L, 12, 3], BF16)
    qs = A("qs", [128, 2, TT], F32)
    sqb = A("sqb", [128, 2, TT], BF16)
    junk = sqb[:].rearrange("p a b -> p (a b)")
    rn = A("rn", [128, 2, TT], F32)
    osq = sqb[:, 0, :]
    orn = rn[:, 0, :]
    ot = qs[:, 0, :]
    qT = A("qT", [128, 4, TT], BF16)
    kT = A("kT", [128, 4, TT], BF16)
    vT = A("vT", [128, 4, TT], BF16)
    vtok = A("vtok", [128, NG, 512], BF16)
    kdtok = A("kdtok", [128, NG, 512], BF16)
    ffT = A("ffT", [128, 4, TT], BF16)
    gsT = A("gsT", [128, 4, TT], BF16)
    ba = A("ba", [128, NG, 8], F32)
    beta = A("beta", [128, NG, 4], F32)
    zz = A("zz", [128, NG, 4], F32)
    la = A("la", [128, NG, 4], F32)
    gam = A("gam", [128, NG, 4], F32)
    ngam = A("ngam", [128, NG, 4], F32)
    kds = A("kds", [128, NG, 4], F32)
    cdv = A("cdv", [128, NG, 4], F32)
    labc = A("labc", [128, 4, 128], F32)
    Gs = A("Gs", [128, 4, 128], F32)
    eG = A("eG", [128, 4, 128], F32)
    tmpD = A("tmpD", [128, 4, 128], F32)
    DT = A("DT", [128, 4, 128], F32)
    Ua = A("Ua", [128, 4, 128], F32)
    B2 = A("B2", [128, 4, 128], BF16)
    A2 = A("A2", [128, 2, 4, 128], BF16)
    Am = A("Am", [128, 2, 4, 128], BF16)
    Tb = A("Tb", [128, 4, 128], BF16)
    Zb = A("Zb", [128, 4, 128], BF16)
    U = A("U", [128, 2, 4, 128], BF16)
    qdT = A("qdT", [128, 2, 4, 128], BF16)
    kgT = A("kgT", [128, 2, 4, 128], BF16)
    qkm = A("qkm", [128, 2, 4, 128], BF16)
    Sst = A("Sst", [128, L, 4, 128], F32)
    Sb = A("Sb", [128, L, 4, 128], BF16)
    rr = A("rr", [128, 4, 128], BF16)
    uu = A("uu", [128, 4, 128], BF16)

    pb = [P("pb%d" % i, [128, 512], F32) for i in range(3)]
    ptrs = [P("ptr%d" % i, [128, 1024], BF16) for i in range(2)]
    pm = [P("pm%d" % i, [128, 512], F32) for i in range(3)]
    ctr = {"pb": 0, "q": 0, "tq": 0, "ring": 0, "dg": 0, "alt": 0, "pmr": 0}

    def pbn():
        ctr["pb"] += 1
        return pb[ctr["pb"] % 3]

    def pbF():
        ctr["pb"] += 1
        return pb[ctr["pb"] % 2]

    pbP_banks = [pb[2], pm[0], pm[1]]

    def pbP():
        ctr["pbp"] = ctr.get("pbp", 0) + 1
        return pbP_banks[ctr["pbp"] % 3]

    def pmn():
        ctr["pmr"] += 1
        return pm[ctr["pmr"] % 2]

    def tpn():
        ctr["tq"] += 1
        return ptrs[ctr["tq"] % 2]

    def alt():
        ctr["alt"] += 1
        return "scalar" if ctr["alt"] % 2 else "vector"

    def C(k0, n=128):
        return cst[:, k0:k0 + n]

    def bc4(ap2d):
        return ap2d.unsqueeze(1).to_broadcast([128, 4, 128])

    def bcf(ap4):
        return ap4.unsqueeze(2).to_broadcast([128, 4, 128])

    def v4(ap512):
        return ap512.rearrange("p (h c) -> p h c", h=4)

    ident_f = C(K_ID)
    mincl = C(K_MINCL)
    mneg = C(K_MNEG)
    mstr = C(K_MSTR)
    ones_f = C(K_ONES)

    def dump(name, ap, shape):
        if name not in dumps:
            return
        d = nc.dram_tensor("dbg_" + name, list(shape), ap.dtype, kind="ExternalOutput").ap()
        dump_d[name] = d
        S.dma("sync", d, ap, key="dbg_" + name, outs=["dbg_" + name], ins=[ap])

    S.dma("sync", cst[:], cst_d[:, :], key="c", ins=[])
    S.dma("sync", prm[:], prm_d[:, :, :], key="c", ins=[])
    wsf = tB[:].rearrange("p a b -> p (a b)")
    S.dma("sync", wsf, wsT_d[:, :, :].rearrange("p l c -> p (l c)"), key="c", ins=[])
    for l in range(L):
        S.dma("gpsimd", wba[:, l, :, :],
              wsrc["w_in"][l].rearrange("(k p) c -> p k c", p=128)[:, :, 3072:3080], key="cw", ins=[])
    S.copy("vector", ident_b[:], ident_f)
    S.copy("vector", ones_b[:], ones_f)
    S.ts("vector", lev1n[:], C(K_LEV1), -1.0, None, ALU.mult)
    for l in range(L):
        for h in range(4):
            S.tt("vector", wsTm[:, l, h, :], wsf[:, l * 512 + h * 128: l * 512 + (h + 1) * 128], mincl, ALU.mult)
        S.act(negA[:, l, :], prm[:, l, C_ALOG:C_ALOG + 4], AF.Exp)
        S.ts("vector", negA[:, l, :], negA[:, l, :], -1.0, None, ALU.mult)
    S.memset("gpsimd", halo_cv[:], 0.0)
    S.memset("gpsimd", halo_dn[:], 0.0)
    S.memset("gpsimd", Sst[:], 0.0)
    S.memset("gpsimd", Sb[:], 0.0)

    def slab_views(l, s):
        src, kind, a, b = tab[s]
        w = wsrc[src][l]
        n = b - a
        if kind == "col":
            sv = w.rearrange("(k p) c -> p k c", p=128)[:, :, a:b]
            dv = wq[l * NSL + s][:, 0:8 * n].rearrange("p (k c) -> p k c", k=8)
        else:
            sv = w.rearrange("(k p) c -> p k c", p=128)[:, a:b, :]
            dv = wq[l * NSL + s][:, 0:n * 1024].rearrange("p (k c) -> p k c", k=n)
        return sv, dv

    def prepass(l, ab):
        if l >= n_layers:
            return
        for s in (range(16) if ab == "a" else range(16, NSL)):
            sv, dv = slab_views(l, s)
            S.dma("gpsimd", dv, sv, key="pre%d%s" % (l, ab), outs=["wq%d_%d" % (l, s)], ins=[])

    prepass(0, "a")
    pre_rest = [(0, "b"), (1, "a"), (1, "b")]

    def load_slab(l, s, F=False):
        rg_, nslot, key = (ringF, 4, "rF") if F else (ringP, 2, "rP")
        slot = ctr.get(key, 0) % nslot
        ctr[key] = ctr.get(key, 0) + 1
        src, kind, a, b = tab[s]
        n = b - a
        ne = 8 * n if kind == "col" else n * 1024
        S.dma("sync", rg_[:, slot, 0:ne], wq[l * NSL + s][:, 0:ne], key="%s%d" % (key, slot),
              ins=["wq%d_%d" % (l, s)])
        if kind == "col":
            return rg_[:, slot, 0:ne].rearrange("p (k c) -> p k c", k=8)
        return rg_[:, slot, 0:ne].rearrange("p (k c) -> p k c", k=n)

    def gc(g):
        return slice(g * 128, (g + 1) * 128)

    def rms_stats(xres):
        for g in range(NG):
            S.act(junk, xres[:, g, :], AF.Square, accum_out=ss[:, g:g + 1])
        S.act(sd[:], ss[:], AF.Sqrt, scale=1.0 / D, bias=EPS)
        S.recip(rstd[:], sd[:])

    def norm_to_hT(l, gcol, xres, actT):
        rms_stats(xres)
        for g in range(NG):
            S.ts("vector", hb[:], xres[:, g, :], rstd[:, g:g + 1], None, ALU.mult)
            for half in range(2):
                for kk in range(4):
                    k = half * 4 + kk
                    S.tr(ptrs[half][:, kk * 128:(kk + 1) * 128], hb[:, k * 128:(k + 1) * 128], ident_b[:])
                S.tt("vector", actT[:, half * 4:(half + 1) * 4, gc(g)],
                     ptrs[half][:, 0:512].rearrange("p (k c) -> p k c", k=4),
                     prm[:, l, gcol + half * 4:gcol + half * 4 + 4].unsqueeze(2).to_broadcast([128, 4, 128]),
                     ALU.mult)
                yield

    def fm_chunk(rs, c, actT, bank=None):
        p = (bank or pbn)()
        for k in range(8):
            S.mm(p[:], rs[:, k, c * 128:(c + 1) * 128], actT[:, k, :], start=(k == 0), stop=(k == 7))
        return p

    def diag(wcol):
        ctr["dg"] += 1
        d = dg[:, ctr["dg"] % 8, :]
        S.ts("gpsimd", d, ident_f, wcol, 0.0, ALU.mult, ALU.add)
        return d

    def phase_P(l, xb):
        xres = xres2[:, xb]
        actT = actT2[:, xb]
        yield from norm_to_hT(l, C_MIXG, xres, actT)
        dump("hT", actT.rearrange("p k t -> p (k t)"), [128, 8 * TT])
        rU = load_slab(l, 0)
        rV = load_slab(l, 1)
        for g in range(NG):
            p = pbP()
            for k in range(8):
                S.mm(p[:, 0:256], actT[:, k, gc(g)], rU[:, k, :], start=(k == 0), stop=(k == 7))
            yield
            for k in range(8):
                S.mm(p[:, 256:512], actT[:, k, gc(g)], rV[:, k, :], start=(k == 0), stop=(k == 7))
            S.act(ugu[:, g, :], p[:, 0:256], AF.Gelu)
            S.act(vg[:, g, :], p[:, 256:512], AF.Gelu)
            pq = pm[2]
            for k in range(8):
                S.mm(pq[:, 0:8], actT[:, k, gc(g)], wba[:, l, k, :], start=(k == 0), stop=(k == 7))
            S.copy("vector", ba[:, g, :], pq[:, 0:8])
            yield
        rA = load_slab(l, 2)
        rG = load_slab(l, 3)
        S.copy("gpsimd", ypre[:, :, 0:30], halo_cv[:, l, :, :])
        for c in range(2):
            pa = fm_chunk(rA, c, actT, pbP)
            yield
            pg = fm_chunk(rG, c, actT, pbP)
            S.act(sig[:, c, :], pg[:], AF.Sigmoid)
            S.tt("vector", ypre[:, c, 30:30 + TT], pa[:], sig[:, c, :], ALU.mult)
            yield
        for hh in range(2):
            r5 = load_slab(l, 10 + hh)
            for c2 in range(2):
                p = fm_chunk(r5, c2, actT, pbP)
                S.act(gsT[:, hh * 2 + c2, :], p[:], AF.Silu)
                yield
        for j in range(3):
            S.copy("gpsimd", qkvpre[:, :, 0:3], halo_dn[:, l, j * 4:(j + 1) * 4, :])
            for hh in range(2):
                r = load_slab(l, 4 + j * 2 + hh)
                for c2 in range(2):
                    p = fm_chunk(r, c2, actT, pbP)
                    S.copy(alt(), qkvpre[:, hh * 2 + c2, 3:3 + TT], p[:])
                    yield
            S.copy("gpsimd", halo_dn[:, l, j * 4:(j + 1) * 4, :], qkvpre[:, :, TT:TT + 3])
            for c in range(4):
                cc = j * 4 + c
                pc = pbP()
                for k in range(4):
                    d = diag(prm[:, l, C_DNW + cc * 4 + k:C_DNW + cc * 4 + k + 1])
                    S.mm(pc[:], d, qkvpre[:, c, k:k + TT], start=(k == 0), stop=(k == 3))
                if j == 2:
                    S.act(vT[:, c, :], pc[:], AF.Silu)
                    yield
                    continue
                i2 = c % 2
                S.act(qs[:, i2, :], pc[:], AF.Silu)
                S.act(sqb[:, i2, :], qs[:, i2, :], AF.Square)
                pN = pm[2]
                S.mm(pN[:], ones_b[:], sqb[:, i2, :])
                S.act(rn[:, i2, :], pN[:], AF.Sqrt, bias=EPS)
                yield
                S.recip(rn[:, i2, :], rn[:, i2, :])
                if j == 0:
                    S.stt("vector", qT[:, c, :], qs[:, i2, :], 128.0 ** -0.5, rn[:, i2, :], ALU.mult, ALU.mult)
                else:
                    S.tt("vector", kT[:, c, :], qs[:, i2, :], rn[:, i2, :], ALU.mult)
                yield
        dump("ugu", ugu[:, :, :].rearrange("p g c -> p (g c)"), [128, NG * 256])
        dump("vg", vg[:, :, :].rearrange("p g c -> p (g c)"), [128, NG * 256])
        dump("ypre", ypre[:, :, :].rearrange("p g c -> p (g c)"), [128, 2 * 542])
        dump("ba", ba[:, :, :].rearrange("p g c -> p (g c)"), [128, 32])
        dump("qT", qT[:, :, :].rearrange("p h t -> p (h t)"), [128, 4 * TT])
        dump("kT", kT[:, :, :].rearrange("p h t -> p (h t)"), [128, 4 * TT])
        dump("vT", vT[:, :, :].rearrange("p h t -> p (h t)"), [128, 4 * TT])

    def phase_M(l, xb):
        actT = actT2[:, xb]
        def side_AB():
            for g in range(NG):
                S.add("vector", lambda e, g=g: e.bn_stats(out=st6[:, g, :], in_=vg[:, g, :]),
                      outs=[st6[:, g, :]], ins=[vg[:, g, :]])
                S.add("vector", lambda e, g=g: e.bn_aggr(out=mv[:, g, :], in_=st6[:, g, :]),
                      outs=[mv[:, g, :]], ins=[st6[:, g, :]])
            S.act(sdA[:], mv[:, :, 1], AF.Sqrt, bias=EPS)
            S.recip(rsA[:], sdA[:])
            yield
            for g in range(NG):
                b2 = g % 2
                S.ts("vector", vh[:, b2, :], vg[:, g, :], mv[:, g, 0:1], rsA[:, g:g + 1], ALU.subtract, ALU.mult)
                S.tt("gpsimd", vh[:, b2, :], vh[:, b2, :], prm[:, l, C_GLG:C_GLG + 256], ALU.mult)
                S.tt("gpsimd", vn[:, g, :], vh[:, b2, :], prm[:, l, C_GLB:C_GLB + 256], ALU.add)
                yield
                pS = pm[2]
                for h in range(4):
                    S.mm(pS[:, h * 64:(h + 1) * 64], wsTm[:, l, h, :], vn[:, g, h * 64:(h + 1) * 64])
                S.tt("vector", tA[:, b2, :], pS[:, 0:256], prm[:, l, C_BS:C_BS + 256], ALU.add)
                S.tt("gpsimd", ya[:, b2, :], tA[:, b2, :], ugu[:, g, :], ALU.mult)
                yield
                tp = tpn()
                for c in range(2):
                    S.tr(tp[:, c * 128:(c + 1) * 128], ya[:, b2, c * 128:(c + 1) * 128], ident_b[:])
                S.copy("scalar", actT[:, 0:2, gc(g)], tp[:, 0:256].rearrange("p (c t) -> p c t", c=2))
                yield

            mark("  B")
            for c in range(2):
                pc = pb[2]
                for k in range(31):
                    d = diag(prm[:, l, C_CVW + c * 31 + k:C_CVW + c * 31 + k + 1])
                    S.mm(pc[:], d, ypre[:, c, k:k + TT], start=(k == 0), stop=(k == 30))
                    if k % 8 == 7:
                        yield
                S.act(cacc[:, c, :], pc[:], AF.Identity, bias=prm[:, l, C_CVB + c:C_CVB + c + 1])
                S.act(csq[:, c, :], cacc[:, c, :], AF.Square)
                yield
            S.copy("gpsimd", halo_cv[:, l, :, :], ypre[:, :, TT:TT + 30])
            dump("cacc", cacc[:, :, :].rearrange("p g c -> p (g c)"), [128, 2 * TT])
            pM = pb[2]
            for c in range(2):
                S.mm(pM[:], ones_f, cacc[:, c, :], start=(c == 0), stop=(c == 1))
            S.ts("vector", meanB[:], pM[:], 1.0 / 256, None, ALU.mult)
            yield
            pQ = pb[2]
            for c in range(2):
                S.mm(pQ[:], ones_f, csq[:, c, :], start=(c == 0), stop=(c == 1))
            S.tt("gpsimd", varB[:], meanB[:], meanB[:], ALU.mult)
            S.stt("vector", varB[:], pQ[:], 1.0 / 256, varB[:], ALU.mult, ALU.subtract)
            S.act(varB[:], varB[:], AF.Sqrt, bias=EPS)
            S.recip(varB[:], varB[:])
            yield
            for c in range(2):
                S.tt("vector", tB[:, c, :], cacc[:, c, :], meanB[:], ALU.subtract)
                S.tt("gpsimd", tB[:, c, :], tB[:, c, :], varB[:], ALU.mult)
                S.act(actT[:, 2 + c, :], tB[:, c, :], AF.Silu,
                      scale=prm[:, l, C_CVLG + c:C_CVLG + c + 1], bias=prm[:, l, C_CVLB + c:C_CVLB + c + 1])
                yield


        mark("  C")
        S.act(beta[:], ba[:, :, 0:4], AF.Sigmoid)
        S.tt("vector", zz[:], ba[:, :, 4:8],
             prm[:, l, C_DTB:C_DTB + 4].unsqueeze(1).to_broadcast([128, NG, 4]), ALU.add)
        S.ts("vector", zz[:], zz[:], 30.0, None, ALU.min)
        S.act(zz[:], zz[:], AF.Exp)
        S.act(zz[:], zz[:], AF.Ln, bias=1.0)
        S.tt("vector", la[:], zz[:], negA[:, l, :].unsqueeze(1).to_broadcast([128, NG, 4]), ALU.mult)
        dump("la", la[:, :, :].rearrange("p g c -> p (g c)"), [128, 16])
        dump("beta", beta[:, :, :].rearrange("p g c -> p (g c)"), [128, 16])
        yield

        def prep(g):
            gb = g % 2
            pq = pm[2]
            S.mm(pq[:, 0:4], mincl, la[:, g, :])
            S.copy("vector", gam[:, g, :], pq[:, 0:4])
            S.ts("vector", ngam[:, g, :], pq[:, 0:4], -1.0, None, ALU.mult)
            tp = tpn()
            for h in range(4):
                S.tr(tp[:, h * 128:(h + 1) * 128], vT[:, h, gc(g)], ident_b[:])
            S.copy("scalar", vtok[:, g, :], tp[:, 0:512])
            yield
            S.copy("gpsimd", labc[:], bcf(la[:, g, :]))
            pG = pmn()
            for h in range(4):
                S.mm(pG[:, h * 128:(h + 1) * 128], labc[:, h, :], mincl)
            S.copy("vector", Gs[:].rearrange("p h c -> p (h c)"), pG[:])
            yield
            S.act(eG[:], Gs[:], AF.Exp)
            S.copy("gpsimd", cdv[:, g, :], eG[:, :, 127])
            S.tt("gpsimd", qdT[:, gb], qT[:, :, gc(g)], eG[:], ALU.mult)
            S.tt("gpsimd", kgT[:, gb], kT[:, :, gc(g)], eG[:], ALU.mult)
            S.tt("vector", tmpD[:], Gs[:], bc4(mneg), ALU.add)
            S.tt("vector", tmpD[:], tmpD[:], bcf(ngam[:, g, :]), ALU.add)
            S.act(DT[:], tmpD[:], AF.Exp)
            S.tt("vector", kds[:, g, :], Gs[:, :, 127], gam[:, g, :], ALU.subtract)
            S.act(kds[:, g, :], kds[:, g, :], AF.Exp)
            yield
            pK = pmn()
            for h in range(4):
                S.mm(pK[:, h * 128:(h + 1) * 128], kT[:, h, gc(g)], kT[:, h, gc(g)])
            S.tt("vector", tmpD[:], v4(pK[:]), DT[:], ALU.mult)
            S.tt("gpsimd", tmpD[:], tmpD[:], bcf(beta[:, g, :]), ALU.mult)
            S.tt("gpsimd", B2[:], tmpD[:], bc4(mstr), ALU.mult)
            yield
            pQK = pmn()
            for h in range(4):
                S.mm(pQK[:, h * 128:(h + 1) * 128], kT[:, h, gc(g)], qT[:, h, gc(g)])
            S.tt("vector", qkm[:, gb], v4(pQK[:]), DT[:], ALU.mult)
            tp = tpn()
            for h in range(4):
                S.tr(tp[:, h * 128:(h + 1) * 128], kT[:, h, gc(g)], ident_b[:])
            S.tt("vector", v4(kdtok[:, g, :]), v4(tp[:, 0:512]), bcf(kds[:, g, :]), ALU.mult)
            yield
            tp = tpn()
            for h in range(4):
                S.tr(tp[:, h * 128:(h + 1) * 128], B2[:, h, :], ident_b[:])
            S.copy("scalar", A2[:, gb].rearrange("p h c -> p (h c)"), tp[:, 0:512])
            S.tt("gpsimd", Ua[:], B2[:], bc4(lev1n[:]), ALU.mult)
            S.tt("gpsimd", U[:, gb], Ua[:], bc4(ident_f), ALU.add)
            yield

        def chain_levels(g):
            gb = g % 2
            for lev in range(6):
                S.tt("gpsimd", Am[:, lev % 2], A2[:, gb], bc4(C(K_LEVT + lev * 128)), ALU.mult)
                tp = tpn()
                for h in range(4):
                    S.tr(tp[:, h * 128:(h + 1) * 128], U[:, gb, h, :], ident_b[:])
                S.copy("scalar", Tb[:].rearrange("p h c -> p (h c)"), tp[:, 0:512])
                yield
                pZ = pmn()
                for h in range(4):
                    S.mm(pZ[:, h * 128:(h + 1) * 128], Am[:, lev % 2, h, :], U[:, gb, h, :])
                S.copy("vector", Zb[:].rearrange("p h c -> p (h c)"), pZ[:])
                yield
                pW = pmn()
                for h in range(4):
                    S.mm(pW[:, h * 128:(h + 1) * 128], Tb[:, h, :], Zb[:, h, :])
                S.tt("vector", U[:, gb], U[:, gb], v4(pW[:]), ALU.subtract)
                yield

        def seq(g):
            gb = g % 2
            pr = pmn()
            for h in range(4):
                S.mm(pr[:, h * 128:(h + 1) * 128], kgT[:, gb, h, :], Sb[:, l, h, :])
            S.tt("vector", rr[:].rearrange("p h c -> p (h c)"), vtok[:, g, :], pr[:], ALU.subtract)
            yield
            pu = pmn()
            for h in range(4):
                S.mm(pu[:, h * 128:(h + 1) * 128], U[:, gb, h, :], rr[:, h, :])
            S.tt("vector", uu[:], v4(pu[:]), bcf(beta[:, g, :]), ALU.mult)
            yield
            pO = pm[2]
            for h in range(4):
                S.mm(pO[:, h * 128:(h + 1) * 128], Sb[:, l, h, :], qdT[:, gb, h, :], start=True, stop=False)
                S.mm(pO[:, h * 128:(h + 1) * 128], uu[:, h, :], qkm[:, gb, h, :], start=False, stop=True)
            pSu = pmn()
            for h in range(4):
                S.mm(pSu[:, h * 128:(h + 1) * 128], kdtok[:, g, h * 128:(h + 1) * 128], uu[:, h, :])
            S.tt("gpsimd", Sst[:, l], Sst[:, l], bcf(cdv[:, g, :]), ALU.mult)
            S.tt("vector", Sst[:, l], Sst[:, l], v4(pSu[:]), ALU.add)
            S.copy("gpsimd", Sb[:, l], Sst[:, l])
            yield
            S.act(osq, pO[:], AF.Square)
            pN2 = pmn()
            S.mm(pN2[:], ones_b[:], osq)
            S.act(orn, pN2[:], AF.Sqrt, scale=1.0 / 128, bias=EPS)
            S.recip(orn, orn)
            S.tt("vector", ot, pO[:], orn, ALU.mult)
            S.stt("vector", actT[:, 4:8, gc(g)], v4(ot), prm[:, l, C_DNG:C_DNG + 1], gsT[:, :, gc(g)],
                  ALU.mult, ALU.mult)
            yield

        def chainsafe(gen):
            for _ in gen:
                yield

        def rr_merge(main, sides):
            sides = list(sides)
            for _ in main:
                yield
                while sides:
                    try:
                        next(sides[0])
                        yield
                        break
                    except StopIteration:
                        sides.pop(0)
            for sg_ in sides:
                yield from sg_

        side = side_AB()
        yield from prep(0)
        for g in range(NG):
            sides = []
            if g > 0:
                sides.append(seq(g - 1))
            if g + 1 < NG:
                sides.append(prep(g + 1))
            sides.append(side)
            yield from rr_merge(chain_levels(g), sides)
        yield from seq(NG - 1)
        yield from side
        dump("mixT", actT.rearrange("p k t -> p (k t)"), [128, 8 * TT])

    def phase_Q(l, xb):
        xres = xres2[:, xb]
        actT = actT2[:, xb]
        for hf in range(2):
            ra = load_slab(l, 12 + 2 * hf)
            rb = load_slab(l, 13 + 2 * hf)
            for g in range(NG):
                p = pbn()
                for k in range(8):
                    S.mm(p[:, 0:256], actT[:, k, gc(g)], ra[:, k, :], start=(k == 0), stop=(k == 7))
                for k in range(8):
                    S.mm(p[:, 256:512], actT[:, k, gc(g)], rb[:, k, :], start=(k == 0), stop=(k == 7))
                S.tt("vector", xres[:, g, hf * 512:(hf + 1) * 512], xres[:, g, hf * 512:(hf + 1) * 512], p[:], ALU.add)
                yield
        dump("x1", xres.rearrange("p g d -> p (g d)"), [128, NG * D])
        yield from norm_to_hT(l, C_FFNG, xres, actT)

    def phase_F(l, xb):
        xres = xres2[:, xb]
        actT = actT2[:, xb]
        s = 16
        for sc in range(6):
            nfc = 4 if sc < 5 else 2
            nh = nfc // 2
            for hh in range(nh):
                rg = load_slab(l, s + 2 * hh, F=True)
                ru = load_slab(l, s + 2 * hh + 1, F=True)
                for c2 in range(2):
                    fc = hh * 2 + c2
                    pg = pbF()
                    for k in range(8):
                        S.mm(pg[:], rg[:, k, c2 * 128:(c2 + 1) * 128], actT[:, k, :], start=(k == 0), stop=(k == 7))
                        if k == 3:
                            yield
                    S.act(sg[:, 0, :], pg[:], AF.Silu)
                    yield
                    pu = pbF()
                    for k in range(8):
                        S.mm(pu[:], ru[:, k, c2 * 128:(c2 + 1) * 128], actT[:, k, :], start=(k == 0), stop=(k == 7))
                        if k == 3:
                            yield
                    S.tt("vector", ffT[:, fc, :], pu[:], sg[:, 0, :], ALU.mult)
                    yield
            rd = [load_slab(l, s + 2 * nh + hh, F=True) for hh in range(nh)]
            for g in range(NG):
                for hf in range(2):
                    p = pbF()
                    for fc in range(nfc):
                        S.mm(p[:], ffT[:, fc, gc(g)], rd[fc // 2][:, fc % 2, hf * 512:(hf + 1) * 512],
                             start=(fc == 0), stop=(fc == nfc - 1))
                    S.tt("vector", xres[:, g, hf * 512:(hf + 1) * 512], xres[:, g, hf * 512:(hf + 1) * 512],
                         p[:], ALU.add)
                    yield
            s += 3 * nh
        dump("x2", xres.rearrange("p g d -> p (g d)"), [128, NG * D])

    xv = x_d.rearrange("(t g p) d -> t p g d", g=NG, p=128)
    ov = out_d.rearrange("(t g p) d -> t p g d", g=NG, p=128)

    def phase_X(t):
        S.dma("scalar", xres2[:, t % 2], xv[t], key="xload", ins=[])

    def phase_E(t):
        xres = xres2[:, t % 2]
        S.dma("scalar", fgb, fng_d[:, :], key="fng", ins=[])
        rms_stats(xres)
        for g in range(NG):
            S.stt("vector", xres[:, g, :], xres[:, g, :], rstd[:, g:g + 1], fgb, ALU.mult, ALU.mult)
        S.dma("scalar", ov[t], xres, key="xstore", outs=["out%d" % t])

    S.marks = []

    def mark(name):
        S.marks.append((name, len(S.by_eng["tensor"])))

    def run(gen):
        for _ in gen:
            pass

    def merge(ga, gb, na, nb):
        alive_a = alive_b = True
        while alive_a or alive_b:
            for _ in range(na):
                if alive_a:
                    try:
                        next(ga)
                    except StopIteration:
                        alive_a = False
            for _ in range(nb):
                if alive_b:
                    try:
                        next(gb)
                    except StopIteration:
                        alive_b = False

    jobs = []
    for t0 in range(0, n_tiles, 2):
        ts_ = [t for t in (t0, t0 + 1) if t < n_tiles]
        for l in range(n_layers):
            for t in ts_:
                jobs.append((t, l))
    if not pipeline or n_layers == 0:
        while pre_rest:
            prepass(*pre_rest.pop(0))
        for t in range(n_tiles):
            phase_X(t)
            for l in range(n_layers):
                run(phase_P(l, t % 2))
                if stop == "inproj":
                    continue
                run(phase_M(l, t % 2))
                if stop == "C":
                    continue
                run(phase_Q(l, t % 2))
                run(phase_F(l, t % 2))
            phase_E(t)
    else:
        pending_F = None

        def chain(*gens):
            for g_ in gens:
                yield from g_

        def marked(name, gen):
            mark(name)
            yield from gen
            if pre_rest:
                prepass(*pre_rest.pop(0))

        for (t, l) in jobs:
            if l == 0:
                phase_X(t)
            pm_gen = chain(marked("P%d_%d" % (t, l), phase_P(l, t % 2)), marked("M%d_%d" % (t, l), phase_M(l, t % 2)))
            if pending_F is not None:
                pt, pl = pending_F
                merge(pm_gen, phase_F(pl, pt % 2), MR[0], MR[1])
                if pl == n_layers - 1:
                    phase_E(pt)
            else:
                run(pm_gen)
            mark("Q%d_%d" % (t, l))
            run(phase_Q(l, t % 2))
            if pre_rest:
                prepass(*pre_rest.pop(0))
            pending_F = (t, l)
        while pre_rest:
            prepass(*pre_rest.pop(0))
        mark("Flast")
        pt, pl = pending_F
        run(phase_F(pl, pt % 2))
        phase_E(pt)
        mark("end")

    S.add("sync", lambda e: None, outs=[], ins=["out%d" % t for t in range(n_tiles)] + ["dbg_" + n for n in dump_d])
    stack = contextlib.ExitStack()
    S.finalize(stack, group_keys=("c", "cw") + PREK)
    stack.close()
    return nc, S


def _consts():
    c = np.zeros((128, NCST), np.float32)
    idx = np.arange(128)
    r, q = idx[:, None], idx[None, :]
    c[:, K_ID:K_ID + 128] = (r == q)
    c[:, K_MINCL:K_MINCL + 128] = (r <= q)
    c[:, K_MNEG:K_MNEG + 128] = np.where(q >= r, 0.0, -30000.0)
    c[:, K_MSTR:K_MSTR + 128] = (r < q)
    c[:, K_LEV1:K_LEV1 + 128] = (r < q) & (r // 2 == q // 2)
    c[:, K_ONES:K_ONES + 128] = 1.0
    for lev in range(6):
        b = 2 << lev
        c[:, K_LEVT + lev * 128:K_LEVT + (lev + 1) * 128] = (q < r) & (r // (2 * b) == q // (2 * b)) & (r // b != q // b)
    return c


def _pack(inp):
    f = lambda a: np.asarray(a, np.float32)
    prm = np.zeros((128, L, NP), np.float32)
    wsT = np.zeros((128, L, 512), np.float32)
    for l in range(L):
        prm[:, l, C_MIXG:C_MIXG + 8] = f(inp["mix_norm_g"])[l].reshape(8, 128).T
        prm[:, l, C_FFNG:C_FFNG + 8] = f(inp["ffn_norm_g"])[l].reshape(8, 128).T
        prm[:, l, C_CVW:C_CVW + 62] = f(inp["cv_dw_w"])[l].reshape(31, 2, 128).transpose(2, 1, 0).reshape(128, 62)
        prm[:, l, C_CVB:C_CVB + 2] = f(inp["cv_dw_b"])[l].reshape(2, 128).T
        prm[:, l, C_CVLG:C_CVLG + 2] = f(inp["cv_ln_g"])[l].reshape(2, 128).T
        prm[:, l, C_CVLB:C_CVLB + 2] = f(inp["cv_ln_b"])[l].reshape(2, 128).T
        prm[:, l, C_DNW:C_DNW + 48] = f(inp["dn_conv_w"])[l].reshape(4, 12, 128).transpose(2, 1, 0).reshape(128, 48)
        prm[:, l, C_DNG] = f(inp["dn_norm_g"])[l]
        prm[:, l, C_ALOG:C_ALOG + 4] = f(inp["dn_a_log"])[l][None, :]
        prm[:, l, C_DTB:C_DTB + 4] = f(inp["dn_dt_bias"])[l][None, :]
        prm[:, l, C_BS:C_BS + 256] = np.repeat(f(inp["gm_bs"])[l].T, 64, axis=1)
        prm[:, l, C_GLG:C_GLG + 256] = f(inp["gm_ln_g"])[l][None, :]
        prm[:, l, C_GLB:C_GLB + 256] = f(inp["gm_ln_b"])[l][None, :]
        wsT[:, l, :] = f(inp["gm_ws"])[l].transpose(2, 0, 1).reshape(128, 512)
    fng = np.ascontiguousarray(np.broadcast_to(f(inp["final_norm_g"])[None, :], (128, D)))
    return prm, wsT, fng


_CACHE = {}


def kernel(**inp):
    x = np.ascontiguousarray(np.asarray(inp["x"], np.float32))
    B = x.shape[0]
    if "nc" not in _CACHE:
        _CACHE["nc"] = build(n_tiles=x.shape[1] // TT, n_layers=L)[0]
    nc = _CACHE["nc"]
    prm, wsT, fng = _pack(inp)
    shared = {
        "w_in": np.ascontiguousarray(np.asarray(inp["w_in"], np.float32)),
        "w_out": np.ascontiguousarray(np.asarray(inp["w_out"], np.float32)),
        "w_gate": np.ascontiguousarray(np.asarray(inp["w_gate"], np.float32)),
        "w_up": np.ascontiguousarray(np.asarray(inp["w_up"], np.float32)),
        "w_down": np.ascontiguousarray(np.asarray(inp["w_down"], np.float32)),
        "params": prm, "wsT": wsT, "fng": fng, "consts": _consts(),
    }
    in_maps = [dict(shared, x=x[b]) for b in range(B)]
    res = run_bass_kernel_spmd(nc, in_maps, core_ids=list(range(B)))
    return np.stack([np.asarray(r["out"], np.float32) for r in res.results], axis=0)
```

```python
import contextlib
import numpy as np
import concourse.bass as bass
import concourse.mybir as mybir
from concourse.bass_utils import run_bass_kernel_spmd

F32 = mybir.dt.float32
BF16 = mybir.dt.bfloat16
AF = mybir.ActivationFunctionType
ALU = mybir.AluOpType

ENGS = ["tensor", "vector", "scalar", "gpsimd", "sync"]
EPOCH = 1500
_ESZ = {F32: 4, BF16: 2}


class _Op:
    __slots__ = ("eng", "fn", "deps", "needed", "sem", "val", "dma", "idx")

    def __init__(self, eng, fn, dma):
        self.eng = eng
        self.fn = fn
        self.deps = []
        self.needed = False
        self.sem = None
        self.val = 0
        self.dma = dma


class _Rec:
    __slots__ = ("plo", "phi", "iv", "lo", "hi", "writer", "readers", "dreaders")

    def __init__(self, reg, writer):
        _, self.plo, self.phi, self.iv = reg
        self.lo = self.iv[0][0]
        self.hi = max(b for _, b in self.iv)
        self.writer = writer
        self.readers = {}
        self.dreaders = []


def _expand(dims, base, esz):
    dims = [(s, c) for s, c in dims if c > 1 and s != 0]
    if not dims:
        return [(base * esz, (base + 1) * esz)]
    dims.sort(key=lambda d: -abs(d[0]))
    starts = [base]
    i = 0
    while i < len(dims):
        s, c = dims[i]
        inner = 1 + sum(abs(ss) * (cc - 1) for ss, cc in dims[i + 1:])
        if s >= inner and len(starts) * c <= 64 and i < len(dims) - 1:
            starts = [st + s * j for st in starts for j in range(c)]
            i += 1
        else:
            break
    ext = 1 + sum(abs(ss) * (cc - 1) for ss, cc in dims[i:])
    return [(st * esz, (st + ext) * esz) for st in starts]


class Sched:
    def __init__(self, nc):
        self.nc = nc
        self.ops = []
        self.by_eng = {e: [] for e in ENGS}
        self.hist = {}
        self.groups = {}
        self._psreaders = set()
        self.unordered = set()

    def _region(self, a):
        if isinstance(a, str):
            return (a, 0, 1, [(0, 1)])
        if str(a.space) == "PSUM":
            return (a.tensor.name, 0, 128, [(0, 2048)])
        pat = a.ap
        off = a.offset
        esz = _ESZ[a.dtype]
        row = pat[0][0]
        if row == 0:
            row = 1 << 40
        plo = off // row
        phi = plo + pat[0][1]
        f0 = off % row
        iv = _expand(list(pat[1:]), f0, esz)
        return (a.tensor.name, plo, phi, iv)

    @staticmethod
    def _overlap(rec, reg):
        _, plo, phi, iv = reg
        if plo >= rec.phi or phi <= rec.plo:
            return False
        for lo, hi in iv:
            if lo >= rec.hi or hi <= rec.lo:
                continue
            for rlo, rhi in rec.iv:
                if lo < rhi and rlo < hi:
                    return True
        return False

    @staticmethod
    def _covers(reg, rec):
        _, plo, phi, iv = reg
        if plo > rec.plo or phi < rec.phi:
            return False
        for rlo, rhi in rec.iv:
            ok = False
            for lo, hi in iv:
                if lo <= rlo and hi >= rhi:
                    ok = True
                    break
            if not ok:
                return False
        return True

    def add(self, eng, fn, outs=(), ins=(), dma=None):
        op = _Op(eng, fn, dma)
        deps = {}
        outs = [a for a in outs if a is not None]
        psr = []
        for a in ins:
            if a is None or isinstance(a, (int, float)):
                continue
            if not isinstance(a, str) and str(a.space) == "PSUM":
                psr.append(a)
                continue
            reg = self._region(a)
            for rec in self.hist.get(reg[0], ()):
                if self._overlap(rec, reg):
                    if rec.writer is not None:
                        deps[rec.writer] = "raw"
                    if dma is not None:
                        rec.dreaders.append(op)
                    else:
                        rec.readers[eng] = op
        for a in list(outs) + psr:
            reg = self._region(a)
            is_psr = any(a is b for b in psr)
            recs = self.hist.get(reg[0], [])
            keep = []
            for rec in recs:
                if self._overlap(rec, reg):
                    if rec.writer is not None and rec.writer is not op:
                        if not (is_psr and rec.writer.eng == eng and rec.writer in self._psreaders):
                            deps.setdefault(rec.writer, "raw" if is_psr else "waw")
                    for r in list(rec.readers.values()) + rec.dreaders:
                        if r is not op:
                            deps.setdefault(r, "war")
                    if self._covers(reg, rec):
                        continue
                keep.append(rec)
            keep.append(_Rec(reg, op))
            self.hist[reg[0]] = keep
            if is_psr and not any(str(getattr(b, "space", "")) == "PSUM" for b in outs):
                self._psreaders.add(op)
        for d, kind in deps.items():
            if d.dma is None and dma is None and d.eng == eng and eng == "tensor":
                continue
            op.deps.append(d)
            d.needed = True
        self.ops.append(op)
        self.by_eng[eng].append(op)
        if dma is not None:
            grp = self.groups.setdefault(dma, [])
            if grp and dma not in self.unordered and grp[-1] not in op.deps:
                op.deps.append(grp[-1])
            grp.append(op)
        return op

    def mm(self, out, lhsT, rhs, start=True, stop=True):
        return self.add("tensor", lambda e: e.matmul(out, lhsT=lhsT, rhs=rhs, start=start, stop=stop),
                        outs=[out], ins=[lhsT, rhs])

    def tr(self, out, in_, ident):
        return self.add("tensor", lambda e: e.transpose(out, in_, ident), outs=[out], ins=[in_, ident])

    def act(self, out, in_, func, bias=None, scale=None, accum_out=None, eng="scalar"):
        kw = {}
        if bias is not None:
            kw["bias"] = bias
        if scale is not None:
            kw["scale"] = scale
        if accum_out is not None:
            kw["accum_out"] = accum_out
        ins = [in_] + [v for v in (bias, scale) if not isinstance(v, (int, float)) and v is not None]
        return self.add(eng, lambda e: e.activation(out=out, in_=in_, func=func, **kw),
                        outs=[out, accum_out], ins=ins)

    def tt(self, eng, out, in0, in1, op):
        return self.add(eng, lambda e: e.tensor_tensor(out=out, in0=in0, in1=in1, op=op),
                        outs=[out], ins=[in0, in1])

    def ts(self, eng, out, in0, s1, s2=None, op0=ALU.mult, op1=None):
        kw = {}
        if op1 is not None:
            kw["op1"] = op1
        return self.add(eng, lambda e: e.tensor_scalar(out=out, in0=in0, scalar1=s1, scalar2=s2, op0=op0, **kw),
                        outs=[out], ins=[in0, s1, s2])

    def stt(self, eng, out, in0, scalar, in1, op0, op1):
        return self.add(eng, lambda e: e.scalar_tensor_tensor(out=out, in0=in0, scalar=scalar, in1=in1,
                                                              op0=op0, op1=op1),
                        outs=[out], ins=[in0, scalar, in1])

    def copy(self, eng, out, in_):
        if eng == "scalar":
            return self.add(eng, lambda e: e.activation(out=out, in_=in_, func=AF.Copy), outs=[out], ins=[in_])
        return self.add(eng, lambda e: e.tensor_copy(out=out, in_=in_), outs=[out], ins=[in_])

    def recip(self, out, in_):
        return self.add("vector", lambda e: e.reciprocal(out=out, in_=in_), outs=[out], ins=[in_])

    def memset(self, eng, out, val):
        return self.add(eng, lambda e: e.memset(out, val), outs=[out], ins=[])

    def dma(self, eng, out, in_, key, outs=None, ins=None):
        return self.add(eng, lambda e: e.dma_start(out=out, in_=in_),
                        outs=outs if outs is not None else [out],
                        ins=ins if ins is not None else [in_], dma=key)

    def finalize(self, stack, group_keys=()):
        nc = self.nc
        nsem = [0]

        def newsem():
            nsem[0] += 1
            return stack.enter_context(nc.semaphore("s%d" % nsem[0]))

        dsem = {}
        dcnt = {}
        for key, ops in self.groups.items():
            dsem[key] = newsem()
            dcnt[key] = 0
        for op in self.ops:
            if op.dma is not None:
                dcnt[op.dma] += 16
                op.sem = dsem[op.dma]
                op.val = dcnt[op.dma]
        for key in group_keys:
            for op in self.groups.get(key, ()):
                op.val = dcnt[key]
        for eng in ENGS:
            sem = None
            cnt = 0
            for op in self.by_eng[eng]:
                if op.dma is not None or not op.needed:
                    continue
                if sem is None or cnt >= EPOCH:
                    sem = newsem()
                    cnt = 0
                cnt += 1
                op.sem = sem
                op.val = cnt
        self.nsem = nsem[0]
        block = stack.enter_context(nc.Block())
        for eng in ENGS:
            ops = self.by_eng[eng]
            if not ops:
                continue

            def body(e, ops=ops):
                waited = {}
                for op in ops:
                    for d in op.deps:
                        k = id(d.sem)
                        if waited.get(k, 0) < d.val:
                            e.wait_ge(d.sem, d.val)
                            waited[k] = d.val
                    ins = op.fn(e)
                    if ins is not None and op.sem is not None:
                        ins.then_inc(op.sem, 16 if op.dma is not None else 1)

            getattr(block, eng)(body)


D = 1024
DPROJ = 3080
DFF = 2816
L = 2
TT = 512
NG = 4
EPS = 1e-6
NSLOT = 4
NP = 909
C_MIXG, C_FFNG, C_CVW, C_CVB, C_CVLG, C_CVLB, C_DNW, C_DNG, C_ALOG, C_DTB, C_BS, C_GLG, C_GLB = (
    0, 8, 16, 78, 80, 82, 84, 132, 133, 137, 141, 397, 653)
K_ID, K_MINCL, K_MNEG, K_MSTR, K_LEV1, K_ONES, K_LEVT = 0, 128, 256, 384, 512, 640, 768
NCST = 768 + 6 * 128


def _slab_table():
    tab = []
    for j in range(12):
        tab.append(("w_in", "col", j * 256, (j + 1) * 256))
    for j in range(4):
        tab.append(("w_out", "col", j * 256, (j + 1) * 256))
    for sc in range(6):
        nfc = 4 if sc < 5 else 2
        for hh in range(nfc // 2):
            c0 = sc * 512 + hh * 256
            tab.append(("w_gate", "col", c0, c0 + 256))
            tab.append(("w_up", "col", c0, c0 + 256))
        for hh in range(nfc // 2):
            k0 = sc * 4 + hh * 2
            tab.append(("w_down", "row", k0, k0 + 2))
    return tab


MR = [1, 1]


def build(n_tiles=8, n_layers=2, dumps=(), stop=None, pipeline=True):
    nc = bass.Bass("TRN2", target_bir_lowering=False)
    NTOK = n_tiles * TT
    x_d = nc.dram_tensor("x", [NTOK, D], F32, kind="ExternalInput").ap()
    wsrc = {
        "w_in": nc.dram_tensor("w_in", [L, D, DPROJ], F32, kind="ExternalInput").ap(),
        "w_out": nc.dram_tensor("w_out", [L, D, D], F32, kind="ExternalInput").ap(),
        "w_gate": nc.dram_tensor("w_gate", [L, D, DFF], F32, kind="ExternalInput").ap(),
        "w_up": nc.dram_tensor("w_up", [L, D, DFF], F32, kind="ExternalInput").ap(),
        "w_down": nc.dram_tensor("w_down", [L, DFF, D], F32, kind="ExternalInput").ap(),
    }
    prm_d = nc.dram_tensor("params", [128, L, NP], F32, kind="ExternalInput").ap()
    wsT_d = nc.dram_tensor("wsT", [128, L, 512], F32, kind="ExternalInput").ap()
    fng_d = nc.dram_tensor("fng", [128, D], F32, kind="ExternalInput").ap()
    cst_d = nc.dram_tensor("consts", [128, NCST], F32, kind="ExternalInput").ap()
    out_d = nc.dram_tensor("out", [NTOK, D], F32, kind="ExternalOutput").ap()
    tab = _slab_table()
    NSL = len(tab)
    wq = nc.dram_tensor("wq", [L * NSL, 128, 2048], BF16, kind="Internal").ap()
    dump_d = {}

    S = Sched(nc)
    PREK = tuple("pre%d%s" % (l, ab) for l in range(L) for ab in "ab")
    S.unordered = set(("c", "cw") + PREK)
    A = nc.alloc_sbuf_tensor
    P = nc.alloc_psum_tensor

    xres2 = A("xres", [128, 2, NG, D], F32)
    actT2 = A("actT", [128, 2, 8, TT], BF16)
    hb = A("hb", [128, D], BF16)
    ringF = A("ringF", [128, 4, 2048], BF16)
    ringP = A("ringP", [128, 2, 2048], BF16)
    cst = A("cst", [128, NCST], F32)
    prm = A("prm", [128, L, NP], F32)
    ident_b = A("ident_b", [128, 128], BF16)
    ones_b = A("ones_b", [128, 128], BF16)
    lev1n = A("lev1n", [128, 128], F32)
    wsTm = A("wsTm", [128, L, 4, 128], BF16)
    wba = A("wba", [128, L, 8, 8], BF16)
    negA = A("negA", [128, L, 4], F32)
    ss = A("ss", [128, 4], F32)
    sd = A("sd", [128, 4], F32)
    rstd = A("rstd", [128, 4], F32)
    ugu = A("ugu", [128, NG, 256], BF16)
    vg = A("vg", [128, NG, 256], F32)
    vn = A("vn", [128, NG, 256], BF16)
    st6 = A("st6", [128, NG, 6], F32)
    mv = A("mv", [128, NG, 2], F32)
    sdA = A("sdA", [128, 4], F32)
    rsA = A("rsA", [128, 4], F32)
    vh = A("vh", [128, 2, 256], F32)
    tA = A("tA", [128, 2, 256], F32)
    ya = A("ya", [128, 2, 256], BF16)
    ypre = A("ypre", [128, 2, 30 + TT], BF16)
    halo_cv = A("halo_cv", [128, L, 2, 30], BF16)
    sig = A("sig", [128, 2, TT], F32)
    sg = A("sg", [128, 1, TT], F32)
    cacc = A("cacc", [128, 2, TT], F32)
    fgb = cacc[:].rearrange("p a b -> p (a b)")
    tB = A("tB", [128, 2, TT], F32)
    csq = tB
    meanB = A("meanB", [128, TT], F32)
    varB = A("varB", [128, TT], F32)
    dg = A("dg", [128, 8, 128], BF16)
    qkvpre = A("qkvpre", [128, 4, 3 + TT], BF16)
    halo_dn = A("halo_dn", [128, L, 12, 3], BF16)
    qs = A("qs", [128, 2, TT], F32)
    sqb = A("sqb", [128, 2, TT], BF16)
    junk = sqb[:].rearrange("p a b -> p (a b)")
    rn = A("rn", [128, 2, TT], F32)
    osq = sqb[:, 0, :]
    orn = rn[:, 0, :]
    ot = qs[:, 0, :]
    qT = A("qT", [128, 4, TT], BF16)
    kT = A("kT", [128, 4, TT], BF16)
    vT = A("vT", [128, 4, TT], BF16)
    vtok = A("vtok", [128, NG, 512], BF16)
    kdtok = A("kdtok", [128, NG, 512], BF16)
    ffT = A("ffT", [128, 4, TT], BF16)
    gsT = A("gsT", [128, 4, TT], BF16)
    ba = A("ba", [128, NG, 8], F32)
    beta = A("beta", [128, NG, 4], F32)
    zz = A("zz", [128, NG, 4], F32)
    la = A("la", [128, NG, 4], F32)
    gam = A("gam", [128, NG, 4], F32)
    ngam = A("ngam", [128, NG, 4], F32)
    kds = A("kds", [128, NG, 4], F32)
    cdv = A("cdv", [128, NG, 4], F32)
    labc = A("labc", [128, 4, 128], F32)
    Gs = A("Gs", [128, 4, 128], F32)
    eG = A("eG", [128, 4, 128], F32)
    tmpD = A("tmpD", [128, 4, 128], F32)
    DT = A("DT", [128, 4, 128], F32)
    Ua = A("Ua", [128, 4, 128], F32)
    B2 = A("B2", [128, 4, 128], BF16)
    A2 = A("A2", [128, 2, 4, 128], BF16)
    Am = A("Am", [128, 2, 4, 128], BF16)
    Tb = A("Tb", [128, 4, 128], BF16)
    Zb = A("Zb", [128, 4, 128], BF16)
    U = A("U", [128, 2, 4, 128], BF16)
    qdT = A("qdT", [128, 2, 4, 128], BF16)
    kgT = A("kgT", [128, 2, 4, 128], BF16)
    qkm = A("qkm", [128, 2, 4, 128], BF16)
    Sst = A("Sst", [128, L, 4, 128], F32)
    Sb = A("Sb", [128, L, 4, 128], BF16)
    rr = A("rr", [128, 4, 128], BF16)
    uu = A("uu", [128, 4, 128], BF16)

    pb = [P("pb%d" % i, [128, 512], F32) for i in range(3)]
    ptrs = [P("ptr%d" % i, [128, 1024], BF16) for i in range(2)]
    pm = [P("pm%d" % i, [128, 512], F32) for i in range(3)]
    ctr = {"pb": 0, "q": 0, "tq": 0, "ring": 0, "dg": 0, "alt": 0, "pmr": 0}

    def pbn():
        ctr["pb"] += 1
        return pb[ctr["pb"] % 3]

    def pbF():
        ctr["pb"] += 1
        return pb[ctr["pb"] % 2]

    pbP_banks = [pb[2], pm[0], pm[1]]

    def pbP():
        ctr["pbp"] = ctr.get("pbp", 0) + 1
        return pbP_banks[ctr["pbp"] % 3]

    def pmn():
        ctr["pmr"] += 1
        return pm[ctr["pmr"] % 2]

    def tpn():
        ctr["tq"] += 1
        return ptrs[ctr["tq"] % 2]

    def alt():
        ctr["alt"] += 1
        return "scalar" if ctr["alt"] % 2 else "vector"

    def C(k0, n=128):
        return cst[:, k0:k0 + n]

    def bc4(ap2d):
        return ap2d.unsqueeze(1).to_broadcast([128, 4, 128])

    def bcf(ap4):
        return ap4.unsqueeze(2).to_broadcast([128, 4, 128])

    def v4(ap512):
        return ap512.rearrange("p (h c) -> p h c", h=4)

    ident_f = C(K_ID)
    mincl = C(K_MINCL)
    mneg = C(K_MNEG)
    mstr = C(K_MSTR)
    ones_f = C(K_ONES)

    def dump(name, ap, shape):
        if name not in dumps:
            return
        d = nc.dram_tensor("dbg_" + name, list(shape), ap.dtype, kind="ExternalOutput").ap()
        dump_d[name] = d
        S.dma("sync", d, ap, key="dbg_" + name, outs=["dbg_" + name], ins=[ap])

    S.dma("sync", cst[:], cst_d[:, :], key="c", ins=[])
    S.dma("sync", prm[:], prm_d[:, :, :], key="c", ins=[])
    wsf = tB[:].rearrange("p a b -> p (a b)")
    S.dma("sync", wsf, wsT_d[:, :, :].rearrange("p l c -> p (l c)"), key="c", ins=[])
    for l in range(L):
        S.dma("gpsimd", wba[:, l, :, :],
              wsrc["w_in"][l].rearrange("(k p) c -> p k c", p=128)[:, :, 3072:3080], key="cw", ins=[])
    S.copy("vector", ident_b[:], ident_f)
    S.copy("vector", ones_b[:], ones_f)
    S.ts("vector", lev1n[:], C(K_LEV1), -1.0, None, ALU.mult)
    for l in range(L):
        for h in range(4):
            S.tt("vector", wsTm[:, l, h, :], wsf[:, l * 512 + h * 128: l * 512 + (h + 1) * 128], mincl, ALU.mult)
        S.act(negA[:, l, :], prm[:, l, C_ALOG:C_ALOG + 4], AF.Exp)
        S.ts("vector", negA[:, l, :], negA[:, l, :], -1.0, None, ALU.mult)
    S.memset("gpsimd", halo_cv[:], 0.0)
    S.memset("gpsimd", halo_dn[:], 0.0)
    S.memset("gpsimd", Sst[:], 0.0)
    S.memset("gpsimd", Sb[:], 0.0)

    def slab_views(l, s):
        src, kind, a, b = tab[s]
        w = wsrc[src][l]
        n = b - a
        if kind == "col":
            sv = w.rearrange("(k p) c -> p k c", p=128)[:, :, a:b]
            dv = wq[l * NSL + s][:, 0:8 * n].rearrange("p (k c) -> p k c", k=8)
        else:
            sv = w.rearrange("(k p) c -> p k c", p=128)[:, a:b, :]
            dv = wq[l * NSL + s][:, 0:n * 1024].rearrange("p (k c) -> p k c", k=n)
        return sv, dv

    def prepass(l, ab):
        if l >= n_layers:
            return
        for s in (range(16) if ab == "a" else range(16, NSL)):
            sv, dv = slab_views(l, s)
            S.dma("gpsimd", dv, sv, key="pre%d%s" % (l, ab), outs=["wq%d_%d" % (l, s)], ins=[])

    prepass(0, "a")
    pre_rest = [(0, "b"), (1, "a"), (1, "b")]

    def load_slab(l, s, F=False):
        rg_, nslot, key = (ringF, 4, "rF") if F else (ringP, 2, "rP")
        slot = ctr.get(key, 0) % nslot
        ctr[key] = ctr.get(key, 0) + 1
        src, kind, a, b = tab[s]
        n = b - a
        ne = 8 * n if kind == "col" else n * 1024
        S.dma("sync", rg_[:, slot, 0:ne], wq[l * NSL + s][:, 0:ne], key="%s%d" % (key, slot),
              ins=["wq%d_%d" % (l, s)])
        if kind == "col":
            return rg_[:, slot, 0:ne].rearrange("p (k c) -> p k c", k=8)
        return rg_[:, slot, 0:ne].rearrange("p (k c) -> p k c", k=n)

    def gc(g):
        return slice(g * 128, (g + 1) * 128)

    def rms_stats(xres):
        for g in range(NG):
            S.act(junk, xres[:, g, :], AF.Square, accum_out=ss[:, g:g + 1])
        S.act(sd[:], ss[:], AF.Sqrt, scale=1.0 / D, bias=EPS)
        S.recip(rstd[:], sd[:])

    def norm_to_hT(l, gcol, xres, actT):
        rms_stats(xres)
        for g in range(NG):
            S.ts("vector", hb[:], xres[:, g, :], rstd[:, g:g + 1], None, ALU.mult)
            for half in range(2):
                for kk in range(4):
                    k = half * 4 + kk
                    S.tr(ptrs[half][:, kk * 128:(kk + 1) * 128], hb[:, k * 128:(k + 1) * 128], ident_b[:])
                S.tt("vector", actT[:, half * 4:(half + 1) * 4, gc(g)],
                     ptrs[half][:, 0:512].rearrange("p (k c) -> p k c", k=4),
                     prm[:, l, gcol + half * 4:gcol + half * 4 + 4].unsqueeze(2).to_broadcast([128, 4, 128]),
                     ALU.mult)
                yield

    def fm_chunk(rs, c, actT, bank=None):
        p = (bank or pbn)()
        for k in range(8):
            S.mm(p[:], rs[:, k, c * 128:(c + 1) * 128], actT[:, k, :], start=(k == 0), stop=(k == 7))
        return p

    def diag(wcol):
        ctr["dg"] += 1
        d = dg[:, ctr["dg"] % 8, :]
        S.ts("gpsimd", d, ident_f, wcol, 0.0, ALU.mult, ALU.add)
        return d

    def phase_P(l, xb):
        xres = xres2[:, xb]
        actT = actT2[:, xb]
        yield from norm_to_hT(l, C_MIXG, xres, actT)
        dump("hT", actT.rearrange("p k t -> p (k t)"), [128, 8 * TT])
        rU = load_slab(l, 0)
        rV = load_slab(l, 1)
        for g in range(NG):
            p = pbP()
            for k in range(8):
                S.mm(p[:, 0:256], actT[:, k, gc(g)], rU[:, k, :], start=(k == 0), stop=(k == 7))
            yield
            for k in range(8):
                S.mm(p[:, 256:512], actT[:, k, gc(g)], rV[:, k, :], start=(k == 0), stop=(k == 7))
            S.act(ugu[:, g, :], p[:, 0:256], AF.Gelu)
            S.act(vg[:, g, :], p[:, 256:512], AF.Gelu)
            pq = pm[2]
            for k in range(8):
                S.mm(pq[:, 0:8], actT[:, k, gc(g)], wba[:, l, k, :], start=(k == 0), stop=(k == 7))
            S.copy("vector", ba[:, g, :], pq[:, 0:8])
            yield
        rA = load_slab(l, 2)
        rG = load_slab(l, 3)
        S.copy("gpsimd", ypre[:, :, 0:30], halo_cv[:, l, :, :])
        for c in range(2):
            pa = fm_chunk(rA, c, actT, pbP)
            yield
            pg = fm_chunk(rG, c, actT, pbP)
            S.act(sig[:, c, :], pg[:], AF.Sigmoid)
            S.tt("vector", ypre[:, c, 30:30 + TT], pa[:], sig[:, c, :], ALU.mult)
            yield
        for hh in range(2):
            r5 = load_slab(l, 10 + hh)
            for c2 in range(2):
                p = fm_chunk(r5, c2, actT, pbP)
                S.act(gsT[:, hh * 2 + c2, :], p[:], AF.Silu)
                yield
        for j in range(3):
            S.copy("gpsimd", qkvpre[:, :, 0:3], halo_dn[:, l, j * 4:(j + 1) * 4, :])
            for hh in range(2):
                r = load_slab(l, 4 + j * 2 + hh)
                for c2 in range(2):
                    p = fm_chunk(r, c2, actT, pbP)
                    S.copy(alt(), qkvpre[:, hh * 2 + c2, 3:3 + TT], p[:])
                    yield
            S.copy("gpsimd", halo_dn[:, l, j * 4:(j + 1) * 4, :], qkvpre[:, :, TT:TT + 3])
            for c in range(4):
                cc = j * 4 + c
                pc = pbP()
                for k in range(4):
                    d = diag(prm[:, l, C_DNW + cc * 4 + k:C_DNW + cc * 4 + k + 1])
                    S.mm(pc[:], d, qkvpre[:, c, k:k + TT], start=(k == 0), stop=(k == 3))
                if j == 2:
                    S.act(vT[:, c, :], pc[:], AF.Silu)
                    yield
                    continue
                i2 = c % 2
                S.act(qs[:, i2, :], pc[:], AF.Silu)
                S.act(sqb[:, i2, :], qs[:, i2, :], AF.Square)
                pN = pm[2]
                S.mm(pN[:], ones_b[:], sqb[:, i2, :])
                S.act(rn[:, i2, :], pN[:], AF.Sqrt, bias=EPS)
                yield
                S.recip(rn[:, i2, :], rn[:, i2, :])
                if j == 0:
                    S.stt("vector", qT[:, c, :], qs[:, i2, :], 128.0 ** -0.5, rn[:, i2, :], ALU.mult, ALU.mult)
                else:
                    S.tt("vector", kT[:, c, :], qs[:, i2, :], rn[:, i2, :], ALU.mult)
                yield
        dump("ugu", ugu[:, :, :].rearrange("p g c -> p (g c)"), [128, NG * 256])
        dump("vg", vg[:, :, :].rearrange("p g c -> p (g c)"), [128, NG * 256])
        dump("ypre", ypre[:, :, :].rearrange("p g c -> p (g c)"), [128, 2 * 542])
        dump("ba", ba[:, :, :].rearrange("p g c -> p (g c)"), [128, 32])
        dump("qT", qT[:, :, :].rearrange("p h t -> p (h t)"), [128, 4 * TT])
        dump("kT", kT[:, :, :].rearrange("p h t -> p (h t)"), [128, 4 * TT])
        dump("vT", vT[:, :, :].rearrange("p h t -> p (h t)"), [128, 4 * TT])

    def phase_M(l, xb):
        actT = actT2[:, xb]
        def side_AB():
            for g in range(NG):
                S.add("vector", lambda e, g=g: e.bn_stats(out=st6[:, g, :], in_=vg[:, g, :]),
                      outs=[st6[:, g, :]], ins=[vg[:, g, :]])
                S.add("vector", lambda e, g=g: e.bn_aggr(out=mv[:, g, :], in_=st6[:, g, :]),
                      outs=[mv[:, g, :]], ins=[st6[:, g, :]])
            S.act(sdA[:], mv[:, :, 1], AF.Sqrt, bias=EPS)
            S.recip(rsA[:], sdA[:])
            yield
            for g in range(NG):
                b2 = g % 2
                S.ts("vector", vh[:, b2, :], vg[:, g, :], mv[:, g, 0:1], rsA[:, g:g + 1], ALU.subtract, ALU.mult)
                S.tt("gpsimd", vh[:, b2, :], vh[:, b2, :], prm[:, l, C_GLG:C_GLG + 256], ALU.mult)
                S.tt("gpsimd", vn[:, g, :], vh[:, b2, :], prm[:, l, C_GLB:C_GLB + 256], ALU.add)
                yield
                pS = pm[2]
                for h in range(4):
                    S.mm(pS[:, h * 64:(h + 1) * 64], wsTm[:, l, h, :], vn[:, g, h * 64:(h + 1) * 64])
                S.tt("vector", tA[:, b2, :], pS[:, 0:256], prm[:, l, C_BS:C_BS + 256], ALU.add)
                S.tt("gpsimd", ya[:, b2, :], tA[:, b2, :], ugu[:, g, :], ALU.mult)
                yield
                tp = tpn()
                for c in range(2):
                    S.tr(tp[:, c * 128:(c + 1) * 128], ya[:, b2, c * 128:(c + 1) * 128], ident_b[:])
                S.copy("scalar", actT[:, 0:2, gc(g)], tp[:, 0:256].rearrange("p (c t) -> p c t", c=2))
                yield

            mark("  B")
            for c in range(2):
                pc = pb[2]
                for k in range(31):
                    d = diag(prm[:, l, C_CVW + c * 31 + k:C_CVW + c * 31 + k + 1])
                    S.mm(pc[:], d, ypre[:, c, k:k + TT], start=(k == 0), stop=(k == 30))
                    if k % 8 == 7:
                        yield
                S.act(cacc[:, c, :], pc[:], AF.Identity, bias=prm[:, l, C_CVB + c:C_CVB + c + 1])
                S.act(csq[:, c, :], cacc[:, c, :], AF.Square)
                yield
            S.copy("gpsimd", halo_cv[:, l, :, :], ypre[:, :, TT:TT + 30])
            dump("cacc", cacc[:, :, :].rearrange("p g c -> p (g c)"), [128, 2 * TT])
            pM = pb[2]
            for c in range(2):
                S.mm(pM[:], ones_f, cacc[:, c, :], start=(c == 0), stop=(c == 1))
            S.ts("vector", meanB[:], pM[:], 1.0 / 256, None, ALU.mult)
            yield
            pQ = pb[2]
            for c in range(2):
                S.mm(pQ[:], ones_f, csq[:, c, :], start=(c == 0), stop=(c == 1))
            S.tt("gpsimd", varB[:], meanB[:], meanB[:], ALU.mult)
            S.stt("vector", varB[:], pQ[:], 1.0 / 256, varB[:], ALU.mult, ALU.subtract)
            S.act(varB[:], varB[:], AF.Sqrt, bias=EPS)
            S.recip(varB[:], varB[:])
            yield
            for c in range(2):
                S.tt("vector", tB[:, c, :], cacc[:, c, :], meanB[:], ALU.subtract)
                S.tt("gpsimd", tB[:, c, :], tB[:, c, :], varB[:], ALU.mult)
                S.act(actT[:, 2 + c, :], tB[:, c, :], AF.Silu,
                      scale=prm[:, l, C_CVLG + c:C_CVLG + c + 1], bias=prm[:, l, C_CVLB + c:C_CVLB + c + 1])
                yield


        mark("  C")
        S.act(beta[:], ba[:, :, 0:4], AF.Sigmoid)
        S.tt("vector", zz[:], ba[:, :, 4:8],
             prm[:, l, C_DTB:C_DTB + 4].unsqueeze(1).to_broadcast([128, NG, 4]), ALU.add)
        S.ts("vector", zz[:], zz[:], 30.0, None, ALU.min)
        S.act(zz[:], zz[:], AF.Exp)
        S.act(zz[:], zz[:], AF.Ln, bias=1.0)
        S.tt("vector", la[:], zz[:], negA[:, l, :].unsqueeze(1).to_broadcast([128, NG, 4]), ALU.mult)
        dump("la", la[:, :, :].rearrange("p g c -> p (g c)"), [128, 16])
        dump("beta", beta[:, :, :].rearrange("p g c -> p (g c)"), [128, 16])
        yield

        def prep(g):
            gb = g % 2
            pq = pm[2]
            S.mm(pq[:, 0:4], mincl, la[:, g, :])
            S.copy("vector", gam[:, g, :], pq[:, 0:4])
            S.ts("vector", ngam[:, g, :], pq[:, 0:4], -1.0, None, ALU.mult)
            tp = tpn()
            for h in range(4):
                S.tr(tp[:, h * 128:(h + 1) * 128], vT[:, h, gc(g)], ident_b[:])
            S.copy("scalar", vtok[:, g, :], tp[:, 0:512])
            yield
            S.copy("gpsimd", labc[:], bcf(la[:, g, :]))
            pG = pmn()
            for h in range(4):
                S.mm(pG[:, h * 128:(h + 1) * 128], labc[:, h, :], mincl)
            S.copy("vector", Gs[:].rearrange("p h c -> p (h c)"), pG[:])
            yield
            S.act(eG[:], Gs[:], AF.Exp)
            S.copy("gpsimd", cdv[:, g, :], eG[:, :, 127])
            S.tt("gpsimd", qdT[:, gb], qT[:, :, gc(g)], eG[:], ALU.mult)
            S.tt("gpsimd", kgT[:, gb], kT[:, :, gc(g)], eG[:], ALU.mult)
            S.tt("vector", tmpD[:], Gs[:], bc4(mneg), ALU.add)
            S.tt("vector", tmpD[:], tmpD[:], bcf(ngam[:, g, :]), ALU.add)
            S.act(DT[:], tmpD[:], AF.Exp)
            S.tt("vector", kds[:, g, :], Gs[:, :, 127], gam[:, g, :], ALU.subtract)
            S.act(kds[:, g, :], kds[:, g, :], AF.Exp)
            yield
            pK = pmn()
            for h in range(4):
                S.mm(pK[:, h * 128:(h + 1) * 128], kT[:, h, gc(g)], kT[:, h, gc(g)])
            S.tt("vector", tmpD[:], v4(pK[:]), DT[:], ALU.mult)
            S.tt("gpsimd", tmpD[:], tmpD[:], bcf(beta[:, g, :]), ALU.mult)
            S.tt("gpsimd", B2[:], tmpD[:], bc4(mstr), ALU.mult)
            yield
            pQK = pmn()
            for h in range(4):
                S.mm(pQK[:, h * 128:(h + 1) * 128], kT[:, h, gc(g)], qT[:, h, gc(g)])
            S.tt("vector", qkm[:, gb], v4(pQK[:]), DT[:], ALU.mult)
            tp = tpn()
            for h in range(4):
                S.tr(tp[:, h * 128:(h + 1) * 128], kT[:, h, gc(g)], ident_b[:])
            S.tt("vector", v4(kdtok[:, g, :]), v4(tp[:, 0:512]), bcf(kds[:, g, :]), ALU.mult)
            yield
            tp = tpn()
            for h in range(4):
                S.tr(tp[:, h * 128:(h + 1) * 128], B2[:, h, :], ident_b[:])
            S.copy("scalar", A2[:, gb].rearrange("p h c -> p (h c)"), tp[:, 0:512])
            S.tt("gpsimd", Ua[:], B2[:], bc4(lev1n[:]), ALU.mult)
            S.tt("gpsimd", U[:, gb], Ua[:], bc4(ident_f), ALU.add)
            yield

        def chain_levels(g):
            gb = g % 2
            for lev in range(6):
                S.tt("gpsimd", Am[:, lev % 2], A2[:, gb], bc4(C(K_LEVT + lev * 128)), ALU.mult)
                tp = tpn()
                for h in range(4):
                    S.tr(tp[:, h * 128:(h + 1) * 128], U[:, gb, h, :], ident_b[:])
                S.copy("scalar", Tb[:].rearrange("p h c -> p (h c)"), tp[:, 0:512])
                yield
                pZ = pmn()
                for h in range(4):
                    S.mm(pZ[:, h * 128:(h + 1) * 128], Am[:, lev % 2, h, :], U[:, gb, h, :])
                S.copy("vector", Zb[:].rearrange("p h c -> p (h c)"), pZ[:])
                yield
                pW = pmn()
                for h in range(4):
                    S.mm(pW[:, h * 128:(h + 1) * 128], Tb[:, h, :], Zb[:, h, :])
                S.tt("vector", U[:, gb], U[:, gb], v4(pW[:]), ALU.subtract)
                yield

        def seq(g):
            gb = g % 2
            pr = pmn()
            for h in range(4):
                S.mm(pr[:, h * 128:(h + 1) * 128], kgT[:, gb, h, :], Sb[:, l, h, :])
            S.tt("vector", rr[:].rearrange("p h c -> p (h c)"), vtok[:, g, :], pr[:], ALU.subtract)
            yield
            pu = pmn()
            for h in range(4):
                S.mm(pu[:, h * 128:(h + 1) * 128], U[:, gb, h, :], rr[:, h, :])
            S.tt("vector", uu[:], v4(pu[:]), bcf(beta[:, g, :]), ALU.mult)
            yield
            pO = pm[2]
            for h in range(4):
                S.mm(pO[:, h * 128:(h + 1) * 128], Sb[:, l, h, :], qdT[:, gb, h, :], start=True, stop=False)
                S.mm(pO[:, h * 128:(h + 1) * 128], uu[:, h, :], qkm[:, gb, h, :], start=False, stop=True)
            pSu = pmn()
            for h in range(4):
                S.mm(pSu[:, h * 128:(h + 1) * 128], kdtok[:, g, h * 128:(h + 1) * 128], uu[:, h, :])
            S.tt("gpsimd", Sst[:, l], Sst[:, l], bcf(cdv[:, g, :]), ALU.mult)
            S.tt("vector", Sst[:, l], Sst[:, l], v4(pSu[:]), ALU.add)
            S.copy("gpsimd", Sb[:, l], Sst[:, l])
            yield
            S.act(osq, pO[:], AF.Square)
            pN2 = pmn()
            S.mm(pN2[:], ones_b[:], osq)
            S.act(orn, pN2[:], AF.Sqrt, scale=1.0 / 128, bias=EPS)
            S.recip(orn, orn)
            S.tt("vector", ot, pO[:], orn, ALU.mult)
            S.stt("vector", actT[:, 4:8, gc(g)], v4(ot), prm[:, l, C_DNG:C_DNG + 1], gsT[:, :, gc(g)],
                  ALU.mult, ALU.mult)
            yield

        def chainsafe(gen):
            for _ in gen:
                yield

        def rr_merge(main, sides):
            sides = list(sides)
            for _ in main:
                yield 1
                while sides:
                    try:
                        next(sides[0])
                        yield 0
                        break
                    except StopIteration:
                        sides.pop(0)
            for sg_ in sides:
                yield from sg_

        side = side_AB()
        yield from prep(0)
        for g in range(NG):
            sides = []
            if g > 0:
                sides.append(seq(g - 1))
            if g + 1 < NG:
                sides.append(prep(g + 1))
            sides.append(side)
            yield from rr_merge(chain_levels(g), sides)
        yield from seq(NG - 1)
        yield from side
        dump("mixT", actT.rearrange("p k t -> p (k t)"), [128, 8 * TT])

    def phase_Q(l, xb):
        xres = xres2[:, xb]
        actT = actT2[:, xb]
        for hf in range(2):
            ra = load_slab(l, 12 + 2 * hf)
            rb = load_slab(l, 13 + 2 * hf)
            for g in range(NG):
                p = pbn()
                for k in range(8):
                    S.mm(p[:, 0:256], actT[:, k, gc(g)], ra[:, k, :], start=(k == 0), stop=(k == 7))
                for k in range(8):
                    S.mm(p[:, 256:512], actT[:, k, gc(g)], rb[:, k, :], start=(k == 0), stop=(k == 7))
                S.tt("vector", xres[:, g, hf * 512:(hf + 1) * 512], xres[:, g, hf * 512:(hf + 1) * 512], p[:], ALU.add)
                yield
        dump("x1", xres.rearrange("p g d -> p (g d)"), [128, NG * D])
        yield from norm_to_hT(l, C_FFNG, xres, actT)

    def phase_F(l, xb):
        xres = xres2[:, xb]
        actT = actT2[:, xb]
        s = 16
        for sc in range(6):
            nfc = 4 if sc < 5 else 2
            nh = nfc // 2
            for hh in range(nh):
                rg = load_slab(l, s + 2 * hh, F=True)
                ru = load_slab(l, s + 2 * hh + 1, F=True)
                for c2 in range(2):
                    fc = hh * 2 + c2
                    pg = pbF()
                    for k in range(8):
                        S.mm(pg[:], rg[:, k, c2 * 128:(c2 + 1) * 128], actT[:, k, :], start=(k == 0), stop=(k == 7))
                        if k == 3:
                            yield
                    S.act(sg[:, 0, :], pg[:], AF.Silu)
                    yield
                    pu = pbF()
                    for k in range(8):
                        S.mm(pu[:], ru[:, k, c2 * 128:(c2 + 1) * 128], actT[:, k, :], start=(k == 0), stop=(k == 7))
                        if k == 3:
                            yield
                    S.tt("vector", ffT[:, fc, :], pu[:], sg[:, 0, :], ALU.mult)
                    yield
            rd = [load_slab(l, s + 2 * nh + hh, F=True) for hh in range(nh)]
            for g in range(NG):
                for hf in range(2):
                    p = pbF()
                    for fc in range(nfc):
                        S.mm(p[:], ffT[:, fc, gc(g)], rd[fc // 2][:, fc % 2, hf * 512:(hf + 1) * 512],
                             start=(fc == 0), stop=(fc == nfc - 1))
                    S.tt("vector", xres[:, g, hf * 512:(hf + 1) * 512], xres[:, g, hf * 512:(hf + 1) * 512],
                         p[:], ALU.add)
                    yield
            s += 3 * nh
        dump("x2", xres.rearrange("p g d -> p (g d)"), [128, NG * D])

    xv = x_d.rearrange("(t g p) d -> t p g d", g=NG, p=128)
    ov = out_d.rearrange("(t g p) d -> t p g d", g=NG, p=128)

    def phase_X(t):
        S.dma("scalar", xres2[:, t % 2], xv[t], key="xload", ins=[])

    def phase_E(t):
        xres = xres2[:, t % 2]
        S.dma("scalar", fgb, fng_d[:, :], key="fng", ins=[])
        rms_stats(xres)
        for g in range(NG):
            S.stt("vector", xres[:, g, :], xres[:, g, :], rstd[:, g:g + 1], fgb, ALU.mult, ALU.mult)
        S.dma("scalar", ov[t], xres, key="xstore", outs=["out%d" % t])

    S.marks = []

    def mark(name):
        S.marks.append((name, len(S.by_eng["tensor"])))

    def run(gen):
        for _ in gen:
            pass

    def merge(ga, gb, na, nb):
        alive_b = True
        for w in ga:
            for _ in range(1 if w is None else w):
                if alive_b:
                    try:
                        next(gb)
                    except StopIteration:
                        alive_b = False
        if alive_b:
            for _ in gb:
                pass

    jobs = []
    for t0 in range(0, n_tiles, 2):
        ts_ = [t for t in (t0, t0 + 1) if t < n_tiles]
        for l in range(n_layers):
            for t in ts_:
                jobs.append((t, l))
    if not pipeline or n_layers == 0:
        while pre_rest:
            prepass(*pre_rest.pop(0))
        for t in range(n_tiles):
            phase_X(t)
            for l in range(n_layers):
                run(phase_P(l, t % 2))
                if stop == "inproj":
                    continue
                run(phase_M(l, t % 2))
                if stop == "C":
                    continue
                run(phase_Q(l, t % 2))
                run(phase_F(l, t % 2))
            phase_E(t)
    else:
        pending_F = None

        def chain(*gens):
            for g_ in gens:
                yield from g_

        def marked(name, gen):
            mark(name)
            yield from gen
            if pre_rest:
                prepass(*pre_rest.pop(0))

        for (t, l) in jobs:
            if l == 0:
                phase_X(t)
            pm_gen = chain(marked("P%d_%d" % (t, l), phase_P(l, t % 2)), marked("M%d_%d" % (t, l), phase_M(l, t % 2)))
            if pending_F is not None:
                pt, pl = pending_F
                merge(pm_gen, phase_F(pl, pt % 2), MR[0], MR[1])
                if pl == n_layers - 1:
                    phase_E(pt)
            else:
                run(pm_gen)
            mark("Q%d_%d" % (t, l))
            run(phase_Q(l, t % 2))
            if pre_rest:
                prepass(*pre_rest.pop(0))
            pending_F = (t, l)
        while pre_rest:
            prepass(*pre_rest.pop(0))
        mark("Flast")
        pt, pl = pending_F
        run(phase_F(pl, pt % 2))
        phase_E(pt)
        mark("end")

    S.add("sync", lambda e: None, outs=[], ins=["out%d" % t for t in range(n_tiles)] + ["dbg_" + n for n in dump_d])
    stack = contextlib.ExitStack()
    S.finalize(stack, group_keys=("c", "cw") + PREK)
    stack.close()
    return nc, S


def _consts():
    c = np.zeros((128, NCST), np.float32)
    idx = np.arange(128)
    r, q = idx[:, None], idx[None, :]
    c[:, K_ID:K_ID + 128] = (r == q)
    c[:, K_MINCL:K_MINCL + 128] = (r <= q)
    c[:, K_MNEG:K_MNEG + 128] = np.where(q >= r, 0.0, -30000.0)
    c[:, K_MSTR:K_MSTR + 128] = (r < q)
    c[:, K_LEV1:K_LEV1 + 128] = (r < q) & (r // 2 == q // 2)
    c[:, K_ONES:K_ONES + 128] = 1.0
    for lev in range(6):
        b = 2 << lev
        c[:, K_LEVT + lev * 128:K_LEVT + (lev + 1) * 128] = (q < r) & (r // (2 * b) == q // (2 * b)) & (r // b != q // b)
    return c


def _pack(inp):
    f = lambda a: np.asarray(a, np.float32)
    prm = np.zeros((128, L, NP), np.float32)
    wsT = np.zeros((128, L, 512), np.float32)
    for l in range(L):
        prm[:, l, C_MIXG:C_MIXG + 8] = f(inp["mix_norm_g"])[l].reshape(8, 128).T
        prm[:, l, C_FFNG:C_FFNG + 8] = f(inp["ffn_norm_g"])[l].reshape(8, 128).T
        prm[:, l, C_CVW:C_CVW + 62] = f(inp["cv_dw_w"])[l].reshape(31, 2, 128).transpose(2, 1, 0).reshape(128, 62)
        prm[:, l, C_CVB:C_CVB + 2] = f(inp["cv_dw_b"])[l].reshape(2, 128).T
        prm[:, l, C_CVLG:C_CVLG + 2] = f(inp["cv_ln_g"])[l].reshape(2, 128).T
        prm[:, l, C_CVLB:C_CVLB + 2] = f(inp["cv_ln_b"])[l].reshape(2, 128).T
        prm[:, l, C_DNW:C_DNW + 48] = f(inp["dn_conv_w"])[l].reshape(4, 12, 128).transpose(2, 1, 0).reshape(128, 48)
        prm[:, l, C_DNG] = f(inp["dn_norm_g"])[l]
        prm[:, l, C_ALOG:C_ALOG + 4] = f(inp["dn_a_log"])[l][None, :]
        prm[:, l, C_DTB:C_DTB + 4] = f(inp["dn_dt_bias"])[l][None, :]
        prm[:, l, C_BS:C_BS + 256] = np.repeat(f(inp["gm_bs"])[l].T, 64, axis=1)
        prm[:, l, C_GLG:C_GLG + 256] = f(inp["gm_ln_g"])[l][None, :]
        prm[:, l, C_GLB:C_GLB + 256] = f(inp["gm_ln_b"])[l][None, :]
        wsT[:, l, :] = f(inp["gm_ws"])[l].transpose(2, 0, 1).reshape(128, 512)
    fng = np.ascontiguousarray(np.broadcast_to(f(inp["final_norm_g"])[None, :], (128, D)))
    return prm, wsT, fng


_CACHE = {}


def kernel(**inp):
    x = np.ascontiguousarray(np.asarray(inp["x"], np.float32))
    B = x.shape[0]
    if "nc" not in _CACHE:
        _CACHE["nc"] = build(n_tiles=x.shape[1] // TT, n_layers=L)[0]
    nc = _CACHE["nc"]
    prm, wsT, fng = _pack(inp)
    shared = {
        "w_in": np.ascontiguousarray(np.asarray(inp["w_in"], np.float32)),
        "w_out": np.ascontiguousarray(np.asarray(inp["w_out"], np.float32)),
        "w_gate": np.ascontiguousarray(np.asarray(inp["w_gate"], np.float32)),
        "w_up": np.ascontiguousarray(np.asarray(inp["w_up"], np.float32)),
        "w_down": np.ascontiguousarray(np.asarray(inp["w_down"], np.float32)),
        "params": prm, "wsT": wsT, "fng": fng, "consts": _consts(),
    }
    in_maps = [dict(shared, x=x[b]) for b in range(B)]
    res = run_bass_kernel_spmd(nc, in_maps, core_ids=list(range(B)))
    return np.stack([np.asarray(r["out"], np.float32) for r in res.results], axis=0)
```

```python
import contextlib
import numpy as np
import concourse.bass as bass
import concourse.mybir as mybir
from concourse.bass_utils import run_bass_kernel_spmd

F32 = mybir.dt.float32
BF16 = mybir.dt.bfloat16
AF = mybir.ActivationFunctionType
ALU = mybir.AluOpType

ENGS = ["tensor", "vector", "scalar", "gpsimd", "sync"]
EPOCH = 1500
_ESZ = {F32: 4, BF16: 2}


class _Op:
    __slots__ = ("eng", "fn", "deps", "needed", "sem", "val", "dma", "idx")

    def __init__(self, eng, fn, dma):
        self.eng = eng
        self.fn = fn
        self.deps = []
        self.needed = False
        self.sem = None
        self.val = 0
        self.dma = dma


class _Rec:
    __slots__ = ("plo", "phi", "iv", "lo", "hi", "writer", "readers", "dreaders")

    def __init__(self, reg, writer):
        _, self.plo, self.phi, self.iv = reg
        self.lo = self.iv[0][0]
        self.hi = max(b for _, b in self.iv)
        self.writer = writer
        self.readers = {}
        self.dreaders = []


def _expand(dims, base, esz):
    dims = [(s, c) for s, c in dims if c > 1 and s != 0]
    if not dims:
        return [(base * esz, (base + 1) * esz)]
    dims.sort(key=lambda d: -abs(d[0]))
    starts = [base]
    i = 0
    while i < len(dims):
        s, c = dims[i]
        inner = 1 + sum(abs(ss) * (cc - 1) for ss, cc in dims[i + 1:])
        if s >= inner and len(starts) * c <= 64 and i < len(dims) - 1:
            starts = [st + s * j for st in starts for j in range(c)]
            i += 1
        else:
            break
    ext = 1 + sum(abs(ss) * (cc - 1) for ss, cc in dims[i:])
    return [(st * esz, (st + ext) * esz) for st in starts]


class Sched:
    def __init__(self, nc):
        self.nc = nc
        self.ops = []
        self.by_eng = {e: [] for e in ENGS}
        self.hist = {}
        self.groups = {}
        self._psreaders = set()
        self.unordered = set()

    def _region(self, a):
        if isinstance(a, str):
            return (a, 0, 1, [(0, 1)])
        if str(a.space) == "PSUM":
            return (a.tensor.name, 0, 128, [(0, 2048)])
        pat = a.ap
        off = a.offset
        esz = _ESZ[a.dtype]
        row = pat[0][0]
        if row == 0:
            row = 1 << 40
        plo = off // row
        phi = plo + pat[0][1]
        f0 = off % row
        iv = _expand(list(pat[1:]), f0, esz)
        return (a.tensor.name, plo, phi, iv)

    @staticmethod
    def _overlap(rec, reg):
        _, plo, phi, iv = reg
        if plo >= rec.phi or phi <= rec.plo:
            return False
        for lo, hi in iv:
            if lo >= rec.hi or hi <= rec.lo:
                continue
            for rlo, rhi in rec.iv:
                if lo < rhi and rlo < hi:
                    return True
        return False

    @staticmethod
    def _covers(reg, rec):
        _, plo, phi, iv = reg
        if plo > rec.plo or phi < rec.phi:
            return False
        for rlo, rhi in rec.iv:
            ok = False
            for lo, hi in iv:
                if lo <= rlo and hi >= rhi:
                    ok = True
                    break
            if not ok:
                return False
        return True

    def add(self, eng, fn, outs=(), ins=(), dma=None):
        op = _Op(eng, fn, dma)
        deps = {}
        outs = [a for a in outs if a is not None]
        psr = []
        for a in ins:
            if a is None or isinstance(a, (int, float)):
                continue
            if not isinstance(a, str) and str(a.space) == "PSUM":
                psr.append(a)
                continue
            reg = self._region(a)
            for rec in self.hist.get(reg[0], ()):
                if self._overlap(rec, reg):
                    if rec.writer is not None:
                        deps[rec.writer] = "raw"
                    if dma is not None:
                        rec.dreaders.append(op)
                    else:
                        rec.readers[eng] = op
        for a in list(outs) + psr:
            reg = self._region(a)
            is_psr = any(a is b for b in psr)
            recs = self.hist.get(reg[0], [])
            keep = []
            for rec in recs:
                if self._overlap(rec, reg):
                    if rec.writer is not None and rec.writer is not op:
                        if not (is_psr and rec.writer.eng == eng and rec.writer in self._psreaders):
                            deps.setdefault(rec.writer, "raw" if is_psr else "waw")
                    for r in list(rec.readers.values()) + rec.dreaders:
                        if r is not op:
                            deps.setdefault(r, "war")
                    if self._covers(reg, rec):
                        continue
                keep.append(rec)
            keep.append(_Rec(reg, op))
            self.hist[reg[0]] = keep
            if is_psr and not any(str(getattr(b, "space", "")) == "PSUM" for b in outs):
                self._psreaders.add(op)
        for d, kind in deps.items():
            if d.dma is None and dma is None and d.eng == eng and eng == "tensor":
                continue
            op.deps.append(d)
            d.needed = True
        self.ops.append(op)
        self.by_eng[eng].append(op)
        if dma is not None:
            grp = self.groups.setdefault(dma, [])
            if grp and dma not in self.unordered and grp[-1] not in op.deps:
                op.deps.append(grp[-1])
            grp.append(op)
        return op

    def mm(self, out, lhsT, rhs, start=True, stop=True):
        return self.add("tensor", lambda e: e.matmul(out, lhsT=lhsT, rhs=rhs, start=start, stop=stop),
                        outs=[out], ins=[lhsT, rhs])

    def tr(self, out, in_, ident):
        return self.add("tensor", lambda e: e.transpose(out, in_, ident), outs=[out], ins=[in_, ident])

    def act(self, out, in_, func, bias=None, scale=None, accum_out=None, eng="scalar"):
        kw = {}
        if bias is not None:
            kw["bias"] = bias
        if scale is not None:
            kw["scale"] = scale
        if accum_out is not None:
            kw["accum_out"] = accum_out
        ins = [in_] + [v for v in (bias, scale) if not isinstance(v, (int, float)) and v is not None]
        return self.add(eng, lambda e: e.activation(out=out, in_=in_, func=func, **kw),
                        outs=[out, accum_out], ins=ins)

    def tt(self, eng, out, in0, in1, op):
        return self.add(eng, lambda e: e.tensor_tensor(out=out, in0=in0, in1=in1, op=op),
                        outs=[out], ins=[in0, in1])

    def ts(self, eng, out, in0, s1, s2=None, op0=ALU.mult, op1=None):
        kw = {}
        if op1 is not None:
            kw["op1"] = op1
        return self.add(eng, lambda e: e.tensor_scalar(out=out, in0=in0, scalar1=s1, scalar2=s2, op0=op0, **kw),
                        outs=[out], ins=[in0, s1, s2])

    def stt(self, eng, out, in0, scalar, in1, op0, op1):
        return self.add(eng, lambda e: e.scalar_tensor_tensor(out=out, in0=in0, scalar=scalar, in1=in1,
                                                              op0=op0, op1=op1),
                        outs=[out], ins=[in0, scalar, in1])

    def copy(self, eng, out, in_):
        if eng == "scalar":
            return self.add(eng, lambda e: e.activation(out=out, in_=in_, func=AF.Copy), outs=[out], ins=[in_])
        return self.add(eng, lambda e: e.tensor_copy(out=out, in_=in_), outs=[out], ins=[in_])

    def recip(self, out, in_):
        return self.add("vector", lambda e: e.reciprocal(out=out, in_=in_), outs=[out], ins=[in_])

    def memset(self, eng, out, val):
        return self.add(eng, lambda e: e.memset(out, val), outs=[out], ins=[])

    def dma(self, eng, out, in_, key, outs=None, ins=None):
        return self.add(eng, lambda e: e.dma_start(out=out, in_=in_),
                        outs=outs if outs is not None else [out],
                        ins=ins if ins is not None else [in_], dma=key)

    def finalize(self, stack, group_keys=()):
        nc = self.nc
        nsem = [0]

        def newsem():
            nsem[0] += 1
            return stack.enter_context(nc.semaphore("s%d" % nsem[0]))

        dsem = {}
        dcnt = {}
        for key, ops in self.groups.items():
            dsem[key] = newsem()
            dcnt[key] = 0
        for op in self.ops:
            if op.dma is not None:
                dcnt[op.dma] += 16
                op.sem = dsem[op.dma]
                op.val = dcnt[op.dma]
        for key in group_keys:
            for op in self.groups.get(key, ()):
                op.val = dcnt[key]
        for eng in ENGS:
            sem = None
            cnt = 0
            for op in self.by_eng[eng]:
                if op.dma is not None or not op.needed:
                    continue
                if sem is None or cnt >= EPOCH:
                    sem = newsem()
                    cnt = 0
                cnt += 1
                op.sem = sem
                op.val = cnt
        self.nsem = nsem[0]
        block = stack.enter_context(nc.Block())
        for eng in ENGS:
            ops = self.by_eng[eng]
            if not ops:
                continue

            def body(e, ops=ops):
                waited = {}
                for op in ops:
                    for d in op.deps:
                        k = id(d.sem)
                        if waited.get(k, 0) < d.val:
                            e.wait_ge(d.sem, d.val)
                            waited[k] = d.val
                    ins = op.fn(e)
                    if ins is not None and op.sem is not None:
                        ins.then_inc(op.sem, 16 if op.dma is not None else 1)

            getattr(block, eng)(body)


D = 1024
DPROJ = 3080
DFF = 2816
L = 2
TT = 512
NG = 4
EPS = 1e-6
NSLOT = 4
NP = 909
C_MIXG, C_FFNG, C_CVW, C_CVB, C_CVLG, C_CVLB, C_DNW, C_DNG, C_ALOG, C_DTB, C_BS, C_GLG, C_GLB = (
    0, 8, 16, 78, 80, 82, 84, 132, 133, 137, 141, 397, 653)
K_ID, K_MINCL, K_MNEG, K_MSTR, K_LEV1, K_ONES, K_LEVT = 0, 128, 256, 384, 512, 640, 768
NCST = 768 + 6 * 128


def _slab_table():
    tab = []
    for j in range(12):
        tab.append(("w_in", "col", j * 256, (j + 1) * 256))
    for j in range(4):
        tab.append(("w_out", "col", j * 256, (j + 1) * 256))
    for sc in range(6):
        nfc = 4 if sc < 5 else 2
        for hh in range(nfc // 2):
            c0 = sc * 512 + hh * 256
            tab.append(("w_gate", "col", c0, c0 + 256))
            tab.append(("w_up", "col", c0, c0 + 256))
        for hh in range(nfc // 2):
            k0 = sc * 4 + hh * 2
            tab.append(("w_down", "row", k0, k0 + 2))
    return tab


MR = [1, 1]


def build(n_tiles=8, n_layers=2, dumps=(), stop=None, pipeline=True):
    nc = bass.Bass("TRN2", target_bir_lowering=False)
    NTOK = n_tiles * TT
    x_d = nc.dram_tensor("x", [NTOK, D], F32, kind="ExternalInput").ap()
    wsrc = {
        "w_in": nc.dram_tensor("w_in", [L, D, DPROJ], F32, kind="ExternalInput").ap(),
        "w_out": nc.dram_tensor("w_out", [L, D, D], F32, kind="ExternalInput").ap(),
        "w_gate": nc.dram_tensor("w_gate", [L, D, DFF], F32, kind="ExternalInput").ap(),
        "w_up": nc.dram_tensor("w_up", [L, D, DFF], F32, kind="ExternalInput").ap(),
        "w_down": nc.dram_tensor("w_down", [L, DFF, D], F32, kind="ExternalInput").ap(),
    }
    prm_d = nc.dram_tensor("params", [128, L, NP], F32, kind="ExternalInput").ap()
    wsT_d = nc.dram_tensor("wsT", [128, L, 512], F32, kind="ExternalInput").ap()
    fng_d = nc.dram_tensor("fng", [128, D], F32, kind="ExternalInput").ap()
    cst_d = nc.dram_tensor("consts", [128, NCST], F32, kind="ExternalInput").ap()
    out_d = nc.dram_tensor("out", [NTOK, D], F32, kind="ExternalOutput").ap()
    tab = _slab_table()
    NSL = len(tab)
    wq = nc.dram_tensor("wq", [L * NSL, 128, 2048], BF16, kind="Internal").ap()
    dump_d = {}

    S = Sched(nc)
    PREK = tuple("pre%d%s" % (l, ab) for l in range(L) for ab in "ab")
    S.unordered = set(("c", "cw") + PREK)
    A = nc.alloc_sbuf_tensor
    P = nc.alloc_psum_tensor

    xres2 = A("xres", [128, 2, NG, D], F32)
    actT2 = A("actT", [128, 2, 8, TT], BF16)
    hb = A("hb", [128, D], BF16)
    ringF = A("ringF", [128, 4, 2048], BF16)
    ringP = A("ringP", [128, 2, 2048], BF16)
    cst = A("cst", [128, NCST], F32)
    prm = A("prm", [128, L, NP], F32)
    ident_b = A("ident_b", [128, 128], BF16)
    ones_b = A("ones_b", [128, 128], BF16)
    lev1n = A("lev1n", [128, 128], F32)
    wsTm = A("wsTm", [128, L, 4, 128], BF16)
    wba = A("wba", [128, L, 8, 8], BF16)
    negA = A("negA", [128, L, 4], F32)
    ss = A("ss", [128, 4], F32)
    sd = A("sd", [128, 4], F32)
    rstd = A("rstd", [128, 4], F32)
    ugu = A("ugu", [128, NG, 256], BF16)
    vg = A("vg", [128, NG, 256], F32)
    vn = A("vn", [128, NG, 256], BF16)
    st6 = A("st6", [128, NG, 6], F32)
    mv = A("mv", [128, NG, 2], F32)
    sdA = A("sdA", [128, 4], F32)
    rsA = A("rsA", [128, 4], F32)
    vh = A("vh", [128, 2, 256], F32)
    tA = A("tA", [128, 2, 256], F32)
    ya = A("ya", [128, 2, 256], BF16)
    ypre = A("ypre", [128, 2, 30 + TT], BF16)
    halo_cv = A("halo_cv", [128, L, 2, 30], BF16)
    sig = A("sig", [128, 2, TT], F32)
    sg = A("sg", [128, 1, TT], F32)
    cacc = A("cacc", [128, 2, TT], F32)
    fgb = cacc[:].rearrange("p a b -> p (a b)")
    tB = A("tB", [128, 2, TT], F32)
    csq = tB
    meanB = A("meanB", [128, TT], F32)
    varB = A("varB", [128, TT], F32)
    dg = A("dg", [128, 8, 128], BF16)
    qkvpre = A("qkvpre", [128, 4, 3 + TT], BF16)
    halo_dn = A("halo_dn", [128, L, 12, 3], BF16)
    qs = A("qs", [128, 2, TT], F32)
    sqb = A("sqb", [128, 2, TT], BF16)
    junk = sqb[:].rearrange("p a b -> p (a b)")
    rn = A("rn", [128, 2, TT], F32)
    osq = sqb[:, 0, :]
    orn = rn[:, 0, :]
    ot = qs[:, 0, :]
    qT = A("qT", [128, 4, TT], BF16)
    kT = A("kT", [128, 4, TT], BF16)
    vT = A("vT", [128, 4, TT], BF16)
    vtok = A("vtok", [128, NG, 512], BF16)
    kdtok = A("kdtok", [128, NG, 512], BF16)
    ffT = A("ffT", [128, 4, TT], BF16)
    gsT = A("gsT", [128, 4, TT], BF16)
    ba = A("ba", [128, NG, 8], F32)
    beta = A("beta", [128, NG, 4], F32)
    zz = A("zz", [128, NG, 4], F32)
    la = A("la", [128, NG, 4], F32)
    gam = A("gam", [128, NG, 4], F32)
    ngam = A("ngam", [128, NG, 4], F32)
    kds = A("kds", [128, NG, 4], F32)
    cdv = A("cdv", [128, NG, 4], F32)
    labc = A("labc", [128, 4, 128], F32)
    Gs = A("Gs", [128, 4, 128], F32)
    eG = A("eG", [128, 4, 128], F32)
    tmpD = A("tmpD", [128, 4, 128], F32)
    DT = A("DT", [128, 4, 128], F32)
    Ua = A("Ua", [128, 4, 128], F32)
    B2 = A("B2", [128, 4, 128], BF16)
    A2 = A("A2", [128, 2, 4, 128], BF16)
    Am = A("Am", [128, 6, 4, 128], BF16)
    Tb = A("Tb", [128, 4, 128], BF16)
    Zb = A("Zb", [128, 4, 128], BF16)
    U = A("U", [128, 2, 4, 128], BF16)
    qdT = A("qdT", [128, 2, 4, 128], BF16)
    kgT = A("kgT", [128, 2, 4, 128], BF16)
    qkm = A("qkm", [128, 2, 4, 128], BF16)
    Sst = A("Sst", [128, L, 4, 128], F32)
    Sb = A("Sb", [128, L, 4, 128], BF16)
    rr = A("rr", [128, 4, 128], BF16)
    uu = A("uu", [128, 4, 128], BF16)

    pb = [P("pb%d" % i, [128, 512], F32) for i in range(3)]
    ptrs = [P("ptr%d" % i, [128, 1024], BF16) for i in range(2)]
    pm = [P("pm%d" % i, [128, 512], F32) for i in range(3)]
    ctr = {"pb": 0, "q": 0, "tq": 0, "ring": 0, "dg": 0, "alt": 0, "pmr": 0}

    def pbn():
        ctr["pb"] += 1
        return pb[ctr["pb"] % 3]

    def pbF():
        ctr["pb"] += 1
        return pb[ctr["pb"] % 2]

    pbP_banks = [pb[2], pm[0], pm[1]]

    def pbP():
        ctr["pbp"] = ctr.get("pbp", 0) + 1
        return pbP_banks[ctr["pbp"] % 3]

    def pmn():
        ctr["pmr"] += 1
        return pm[ctr["pmr"] % 2]

    def tpn():
        ctr["tq"] += 1
        return ptrs[ctr["tq"] % 2]

    def alt():
        ctr["alt"] += 1
        return "scalar" if ctr["alt"] % 2 else "vector"

    def C(k0, n=128):
        return cst[:, k0:k0 + n]

    def bc4(ap2d):
        return ap2d.unsqueeze(1).to_broadcast([128, 4, 128])

    def bcf(ap4):
        return ap4.unsqueeze(2).to_broadcast([128, 4, 128])

    def v4(ap512):
        return ap512.rearrange("p (h c) -> p h c", h=4)

    ident_f = C(K_ID)
    mincl = C(K_MINCL)
    mneg = C(K_MNEG)
    mstr = C(K_MSTR)
    ones_f = C(K_ONES)

    def dump(name, ap, shape):
        if name not in dumps:
            return
        d = nc.dram_tensor("dbg_" + name, list(shape), ap.dtype, kind="ExternalOutput").ap()
        dump_d[name] = d
        S.dma("sync", d, ap, key="dbg_" + name, outs=["dbg_" + name], ins=[ap])

    S.dma("sync", cst[:], cst_d[:, :], key="c", ins=[])
    S.dma("sync", prm[:], prm_d[:, :, :], key="c", ins=[])
    wsf = tB[:].rearrange("p a b -> p (a b)")
    S.dma("sync", wsf, wsT_d[:, :, :].rearrange("p l c -> p (l c)"), key="c", ins=[])
    for l in range(L):
        S.dma("gpsimd", wba[:, l, :, :],
              wsrc["w_in"][l].rearrange("(k p) c -> p k c", p=128)[:, :, 3072:3080], key="cw", ins=[])
    S.copy("vector", ident_b[:], ident_f)
    S.copy("vector", ones_b[:], ones_f)
    S.ts("vector", lev1n[:], C(K_LEV1), -1.0, None, ALU.mult)
    for l in range(L):
        for h in range(4):
            S.tt("vector", wsTm[:, l, h, :], wsf[:, l * 512 + h * 128: l * 512 + (h + 1) * 128], mincl, ALU.mult)
        S.act(negA[:, l, :], prm[:, l, C_ALOG:C_ALOG + 4], AF.Exp)
        S.ts("vector", negA[:, l, :], negA[:, l, :], -1.0, None, ALU.mult)
    S.memset("gpsimd", halo_cv[:], 0.0)
    S.memset("gpsimd", halo_dn[:], 0.0)
    S.memset("gpsimd", Sst[:], 0.0)
    S.memset("gpsimd", Sb[:], 0.0)

    def slab_views(l, s):
        src, kind, a, b = tab[s]
        w = wsrc[src][l]
        n = b - a
        if kind == "col":
            sv = w.rearrange("(k p) c -> p k c", p=128)[:, :, a:b]
            dv = wq[l * NSL + s][:, 0:8 * n].rearrange("p (k c) -> p k c", k=8)
        else:
            sv = w.rearrange("(k p) c -> p k c", p=128)[:, a:b, :]
            dv = wq[l * NSL + s][:, 0:n * 1024].rearrange("p (k c) -> p k c", k=n)
        return sv, dv

    def prepass(l, ab):
        if l >= n_layers:
            return
        for s in (range(16) if ab == "a" else range(16, NSL)):
            sv, dv = slab_views(l, s)
            S.dma("gpsimd", dv, sv, key="pre%d%s" % (l, ab), outs=["wq%d_%d" % (l, s)], ins=[])

    prepass(0, "a")
    prepass(0, "b")
    pre_rest = [(1, "a"), (1, "b")]

    def load_slab(l, s, F=False):
        rg_, nslot, key = (ringF, 4, "rF") if F else (ringP, 2, "rP")
        slot = ctr.get(key, 0) % nslot
        ctr[key] = ctr.get(key, 0) + 1
        src, kind, a, b = tab[s]
        n = b - a
        ne = 8 * n if kind == "col" else n * 1024
        S.dma("sync", rg_[:, slot, 0:ne], wq[l * NSL + s][:, 0:ne], key="%s%d" % (key, slot),
              ins=["wq%d_%d" % (l, s)])
        if kind == "col":
            return rg_[:, slot, 0:ne].rearrange("p (k c) -> p k c", k=8)
        return rg_[:, slot, 0:ne].rearrange("p (k c) -> p k c", k=n)

    def gc(g):
        return slice(g * 128, (g + 1) * 128)

    def rms_stats(xres):
        for g in range(NG):
            S.act(junk, xres[:, g, :], AF.Square, accum_out=ss[:, g:g + 1])
        S.act(sd[:], ss[:], AF.Sqrt, scale=1.0 / D, bias=EPS)
        S.recip(rstd[:], sd[:])

    def norm_to_hT(l, gcol, xres, actT):
        rms_stats(xres)
        for g in range(NG):
            S.ts("vector", hb[:], xres[:, g, :], rstd[:, g:g + 1], None, ALU.mult)
            for half in range(2):
                for kk in range(4):
                    k = half * 4 + kk
                    S.tr(ptrs[half][:, kk * 128:(kk + 1) * 128], hb[:, k * 128:(k + 1) * 128], ident_b[:])
                S.tt("vector", actT[:, half * 4:(half + 1) * 4, gc(g)],
                     ptrs[half][:, 0:512].rearrange("p (k c) -> p k c", k=4),
                     prm[:, l, gcol + half * 4:gcol + half * 4 + 4].unsqueeze(2).to_broadcast([128, 4, 128]),
                     ALU.mult)
                yield

    def fm_chunk(rs, c, actT, bank=None):
        p = (bank or pbn)()
        for k in range(8):
            S.mm(p[:], rs[:, k, c * 128:(c + 1) * 128], actT[:, k, :], start=(k == 0), stop=(k == 7))
        return p

    def diag(wcol):
        ctr["dg"] += 1
        d = dg[:, ctr["dg"] % 8, :]
        S.ts("gpsimd", d, ident_f, wcol, 0.0, ALU.mult, ALU.add)
        return d

    def phase_P(l, xb):
        xres = xres2[:, xb]
        actT = actT2[:, xb]
        yield from norm_to_hT(l, C_MIXG, xres, actT)
        dump("hT", actT.rearrange("p k t -> p (k t)"), [128, 8 * TT])
        rU = load_slab(l, 0)
        rV = load_slab(l, 1)
        for g in range(NG):
            p = pbP()
            for k in range(8):
                S.mm(p[:, 0:256], actT[:, k, gc(g)], rU[:, k, :], start=(k == 0), stop=(k == 7))
            yield
            for k in range(8):
                S.mm(p[:, 256:512], actT[:, k, gc(g)], rV[:, k, :], start=(k == 0), stop=(k == 7))
            S.act(ugu[:, g, :], p[:, 0:256], AF.Gelu)
            S.act(vg[:, g, :], p[:, 256:512], AF.Gelu)
            pq = pm[2]
            for k in range(8):
                S.mm(pq[:, 0:8], actT[:, k, gc(g)], wba[:, l, k, :], start=(k == 0), stop=(k == 7))
            S.copy("vector", ba[:, g, :], pq[:, 0:8])
            yield
        rA = load_slab(l, 2)
        rG = load_slab(l, 3)
        S.copy("gpsimd", ypre[:, :, 0:30], halo_cv[:, l, :, :])
        for c in range(2):
            pa = fm_chunk(rA, c, actT, pbP)
            yield
            pg = fm_chunk(rG, c, actT, pbP)
            S.act(sig[:, c, :], pg[:], AF.Sigmoid)
            S.tt("vector", ypre[:, c, 30:30 + TT], pa[:], sig[:, c, :], ALU.mult)
            yield
        for hh in range(2):
            r5 = load_slab(l, 10 + hh)
            for c2 in range(2):
                p = fm_chunk(r5, c2, actT, pbP)
                S.act(gsT[:, hh * 2 + c2, :], p[:], AF.Silu)
                yield
        for j in range(3):
            S.copy("gpsimd", qkvpre[:, :, 0:3], halo_dn[:, l, j * 4:(j + 1) * 4, :])
            for hh in range(2):
                r = load_slab(l, 4 + j * 2 + hh)
                for c2 in range(2):
                    p = fm_chunk(r, c2, actT, pbP)
                    S.copy(alt(), qkvpre[:, hh * 2 + c2, 3:3 + TT], p[:])
                    yield
            S.copy("gpsimd", halo_dn[:, l, j * 4:(j + 1) * 4, :], qkvpre[:, :, TT:TT + 3])
            for c in range(4):
                cc = j * 4 + c
                pc = pbP()
                for k in range(4):
                    d = diag(prm[:, l, C_DNW + cc * 4 + k:C_DNW + cc * 4 + k + 1])
                    S.mm(pc[:], d, qkvpre[:, c, k:k + TT], start=(k == 0), stop=(k == 3))
                if j == 2:
                    S.act(vT[:, c, :], pc[:], AF.Silu)
                    yield
                    continue
                i2 = c % 2
                S.act(qs[:, i2, :], pc[:], AF.Silu)
                S.act(sqb[:, i2, :], qs[:, i2, :], AF.Square)
                pN = pm[2]
                S.mm(pN[:], ones_b[:], sqb[:, i2, :])
                S.act(rn[:, i2, :], pN[:], AF.Sqrt, bias=EPS)
                yield
                S.recip(rn[:, i2, :], rn[:, i2, :])
                if j == 0:
                    S.stt("vector", qT[:, c, :], qs[:, i2, :], 128.0 ** -0.5, rn[:, i2, :], ALU.mult, ALU.mult)
                else:
                    S.tt("vector", kT[:, c, :], qs[:, i2, :], rn[:, i2, :], ALU.mult)
                yield
        dump("ugu", ugu[:, :, :].rearrange("p g c -> p (g c)"), [128, NG * 256])
        dump("vg", vg[:, :, :].rearrange("p g c -> p (g c)"), [128, NG * 256])
        dump("ypre", ypre[:, :, :].rearrange("p g c -> p (g c)"), [128, 2 * 542])
        dump("ba", ba[:, :, :].rearrange("p g c -> p (g c)"), [128, 32])
        dump("qT", qT[:, :, :].rearrange("p h t -> p (h t)"), [128, 4 * TT])
        dump("kT", kT[:, :, :].rearrange("p h t -> p (h t)"), [128, 4 * TT])
        dump("vT", vT[:, :, :].rearrange("p h t -> p (h t)"), [128, 4 * TT])

    def phase_M(l, xb):
        actT = actT2[:, xb]
        def side_AB():
            for g in range(NG):
                S.add("vector", lambda e, g=g: e.bn_stats(out=st6[:, g, :], in_=vg[:, g, :]),
                      outs=[st6[:, g, :]], ins=[vg[:, g, :]])
                S.add("vector", lambda e, g=g: e.bn_aggr(out=mv[:, g, :], in_=st6[:, g, :]),
                      outs=[mv[:, g, :]], ins=[st6[:, g, :]])
            S.act(sdA[:], mv[:, :, 1], AF.Sqrt, bias=EPS)
            S.recip(rsA[:], sdA[:])
            yield
            for g in range(NG):
                b2 = g % 2
                S.ts("vector", vh[:, b2, :], vg[:, g, :], mv[:, g, 0:1], rsA[:, g:g + 1], ALU.subtract, ALU.mult)
                S.tt("gpsimd", vh[:, b2, :], vh[:, b2, :], prm[:, l, C_GLG:C_GLG + 256], ALU.mult)
                S.tt("gpsimd", vn[:, g, :], vh[:, b2, :], prm[:, l, C_GLB:C_GLB + 256], ALU.add)
                yield
                pS = pm[2]
                for h in range(4):
                    S.mm(pS[:, h * 64:(h + 1) * 64], wsTm[:, l, h, :], vn[:, g, h * 64:(h + 1) * 64])
                S.tt("vector", tA[:, b2, :], pS[:, 0:256], prm[:, l, C_BS:C_BS + 256], ALU.add)
                S.tt("gpsimd", ya[:, b2, :], tA[:, b2, :], ugu[:, g, :], ALU.mult)
                yield
                tp = tpn()
                for c in range(2):
                    S.tr(tp[:, c * 128:(c + 1) * 128], ya[:, b2, c * 128:(c + 1) * 128], ident_b[:])
                S.copy("scalar", actT[:, 0:2, gc(g)], tp[:, 0:256].rearrange("p (c t) -> p c t", c=2))
                yield

            mark("  B")
            for c in range(2):
                pc = pb[2]
                for k in range(31):
                    d = diag(prm[:, l, C_CVW + c * 31 + k:C_CVW + c * 31 + k + 1])
                    S.mm(pc[:], d, ypre[:, c, k:k + TT], start=(k == 0), stop=(k == 30))
                    if k % 8 == 7:
                        yield
                S.act(cacc[:, c, :], pc[:], AF.Identity, bias=prm[:, l, C_CVB + c:C_CVB + c + 1])
                S.act(csq[:, c, :], cacc[:, c, :], AF.Square)
                yield
            S.copy("gpsimd", halo_cv[:, l, :, :], ypre[:, :, TT:TT + 30])
            dump("cacc", cacc[:, :, :].rearrange("p g c -> p (g c)"), [128, 2 * TT])
            pM = pb[2]
            for c in range(2):
                S.mm(pM[:], ones_f, cacc[:, c, :], start=(c == 0), stop=(c == 1))
            S.ts("vector", meanB[:], pM[:], 1.0 / 256, None, ALU.mult)
            yield
            pQ = pb[2]
            for c in range(2):
                S.mm(pQ[:], ones_f, csq[:, c, :], start=(c == 0), stop=(c == 1))
            S.tt("gpsimd", varB[:], meanB[:], meanB[:], ALU.mult)
            S.stt("vector", varB[:], pQ[:], 1.0 / 256, varB[:], ALU.mult, ALU.subtract)
            S.act(varB[:], varB[:], AF.Sqrt, bias=EPS)
            S.recip(varB[:], varB[:])
            yield
            for c in range(2):
                S.tt("vector", tB[:, c, :], cacc[:, c, :], meanB[:], ALU.subtract)
                S.tt("gpsimd", tB[:, c, :], tB[:, c, :], varB[:], ALU.mult)
                S.act(actT[:, 2 + c, :], tB[:, c, :], AF.Silu,
                      scale=prm[:, l, C_CVLG + c:C_CVLG + c + 1], bias=prm[:, l, C_CVLB + c:C_CVLB + c + 1])
                yield


        mark("  C")
        S.act(beta[:], ba[:, :, 0:4], AF.Sigmoid)
        S.tt("vector", zz[:], ba[:, :, 4:8],
             prm[:, l, C_DTB:C_DTB + 4].unsqueeze(1).to_broadcast([128, NG, 4]), ALU.add)
        S.ts("vector", zz[:], zz[:], 30.0, None, ALU.min)
        S.act(zz[:], zz[:], AF.Exp)
        S.act(zz[:], zz[:], AF.Ln, bias=1.0)
        S.tt("vector", la[:], zz[:], negA[:, l, :].unsqueeze(1).to_broadcast([128, NG, 4]), ALU.mult)
        dump("la", la[:, :, :].rearrange("p g c -> p (g c)"), [128, 16])
        dump("beta", beta[:, :, :].rearrange("p g c -> p (g c)"), [128, 16])
        yield

        def prep(g):
            gb = g % 2
            pq = pm[2]
            S.mm(pq[:, 0:4], mincl, la[:, g, :])
            S.copy("vector", gam[:, g, :], pq[:, 0:4])
            S.ts("vector", ngam[:, g, :], pq[:, 0:4], -1.0, None, ALU.mult)
            tp = tpn()
            for h in range(4):
                S.tr(tp[:, h * 128:(h + 1) * 128], vT[:, h, gc(g)], ident_b[:])
            S.copy("scalar", vtok[:, g, :], tp[:, 0:512])
            yield
            S.copy("gpsimd", labc[:], bcf(la[:, g, :]))
            pG = pmn()
            for h in range(4):
                S.mm(pG[:, h * 128:(h + 1) * 128], labc[:, h, :], mincl)
            S.copy("vector", Gs[:].rearrange("p h c -> p (h c)"), pG[:])
            yield
            S.act(eG[:], Gs[:], AF.Exp)
            S.copy("gpsimd", cdv[:, g, :], eG[:, :, 127])
            S.tt("gpsimd", qdT[:, gb], qT[:, :, gc(g)], eG[:], ALU.mult)
            S.tt("gpsimd", kgT[:, gb], kT[:, :, gc(g)], eG[:], ALU.mult)
            S.tt("vector", tmpD[:], Gs[:], bc4(mneg), ALU.add)
            S.tt("vector", tmpD[:], tmpD[:], bcf(ngam[:, g, :]), ALU.add)
            S.act(DT[:], tmpD[:], AF.Exp)
            S.tt("vector", kds[:, g, :], Gs[:, :, 127], gam[:, g, :], ALU.subtract)
            S.act(kds[:, g, :], kds[:, g, :], AF.Exp)
            yield
            pK = pmn()
            for h in range(4):
                S.mm(pK[:, h * 128:(h + 1) * 128], kT[:, h, gc(g)], kT[:, h, gc(g)])
            S.tt("vector", tmpD[:], v4(pK[:]), DT[:], ALU.mult)
            S.tt("gpsimd", tmpD[:], tmpD[:], bcf(beta[:, g, :]), ALU.mult)
            S.tt("gpsimd", B2[:], tmpD[:], bc4(mstr), ALU.mult)
            yield
            pQK = pmn()
            for h in range(4):
                S.mm(pQK[:, h * 128:(h + 1) * 128], kT[:, h, gc(g)], qT[:, h, gc(g)])
            S.tt("vector", qkm[:, gb], v4(pQK[:]), DT[:], ALU.mult)
            tp = tpn()
            for h in range(4):
                S.tr(tp[:, h * 128:(h + 1) * 128], kT[:, h, gc(g)], ident_b[:])
            S.tt("vector", v4(kdtok[:, g, :]), v4(tp[:, 0:512]), bcf(kds[:, g, :]), ALU.mult)
            yield
            tp = tpn()
            for h in range(4):
                S.tr(tp[:, h * 128:(h + 1) * 128], B2[:, h, :], ident_b[:])
            S.copy("scalar", A2[:, gb].rearrange("p h c -> p (h c)"), tp[:, 0:512])
            S.tt("gpsimd", Ua[:], B2[:], bc4(lev1n[:]), ALU.mult)
            S.tt("gpsimd", U[:, gb], Ua[:], bc4(ident_f), ALU.add)
            yield

        def chain_levels(g):
            gb = g % 2
            for lev in range(6):
                S.tt("gpsimd", Am[:, lev], A2[:, gb], bc4(C(K_LEVT + lev * 128)), ALU.mult)
                if lev % 2 == 1:
                    yield
            for lev in range(6):
                tp = tpn()
                for h in range(4):
                    S.tr(tp[:, h * 128:(h + 1) * 128], U[:, gb, h, :], ident_b[:])
                pZ = pmn()
                for h in range(4):
                    S.mm(pZ[:, h * 128:(h + 1) * 128], Am[:, lev, h, :], U[:, gb, h, :])
                S.copy("scalar", Tb[:].rearrange("p h c -> p (h c)"), tp[:, 0:512])
                S.copy("vector", Zb[:].rearrange("p h c -> p (h c)"), pZ[:])
                yield
                pW = pmn()
                for h in range(4):
                    S.mm(pW[:, h * 128:(h + 1) * 128], Tb[:, h, :], Zb[:, h, :])
                S.tt("vector", U[:, gb], U[:, gb], v4(pW[:]), ALU.subtract)
                yield

        def seq(g):
            gb = g % 2
            pr = pmn()
            for h in range(4):
                S.mm(pr[:, h * 128:(h + 1) * 128], kgT[:, gb, h, :], Sb[:, l, h, :])
            S.tt("vector", rr[:].rearrange("p h c -> p (h c)"), vtok[:, g, :], pr[:], ALU.subtract)
            yield
            pu = pmn()
            for h in range(4):
                S.mm(pu[:, h * 128:(h + 1) * 128], U[:, gb, h, :], rr[:, h, :])
            S.tt("vector", uu[:], v4(pu[:]), bcf(beta[:, g, :]), ALU.mult)
            yield
            pO = pm[2]
            for h in range(4):
                S.mm(pO[:, h * 128:(h + 1) * 128], Sb[:, l, h, :], qdT[:, gb, h, :], start=True, stop=False)
                S.mm(pO[:, h * 128:(h + 1) * 128], uu[:, h, :], qkm[:, gb, h, :], start=False, stop=True)
            pSu = pmn()
            for h in range(4):
                S.mm(pSu[:, h * 128:(h + 1) * 128], kdtok[:, g, h * 128:(h + 1) * 128], uu[:, h, :])
            S.tt("gpsimd", Sst[:, l], Sst[:, l], bcf(cdv[:, g, :]), ALU.mult)
            S.tt("vector", Sst[:, l], Sst[:, l], v4(pSu[:]), ALU.add)
            S.copy("gpsimd", Sb[:, l], Sst[:, l])
            yield
            S.act(osq, pO[:], AF.Square)
            pN2 = pmn()
            S.mm(pN2[:], ones_b[:], osq)
            S.act(orn, pN2[:], AF.Sqrt, scale=1.0 / 128, bias=EPS)
            S.recip(orn, orn)
            S.tt("vector", ot, pO[:], orn, ALU.mult)
            S.stt("vector", actT[:, 4:8, gc(g)], v4(ot), prm[:, l, C_DNG:C_DNG + 1], gsT[:, :, gc(g)],
                  ALU.mult, ALU.mult)
            yield

        def chainsafe(gen):
            for _ in gen:
                yield

        def rr_merge(main, sides):
            sides = list(sides)
            for _ in main:
                yield 1
                while sides:
                    try:
                        next(sides[0])
                        yield 0
                        break
                    except StopIteration:
                        sides.pop(0)
            for sg_ in sides:
                yield from sg_

        side = side_AB()
        yield from prep(0)
        for g in range(NG):
            sides = []
            if g > 0:
                sides.append(seq(g - 1))
            if g + 1 < NG:
                sides.append(prep(g + 1))
            sides.append(side)
            yield from rr_merge(chain_levels(g), sides)
        yield from seq(NG - 1)
        yield from side
        dump("mixT", actT.rearrange("p k t -> p (k t)"), [128, 8 * TT])

    def phase_Q(l, xb):
        xres = xres2[:, xb]
        actT = actT2[:, xb]
        for hf in range(2):
            ra = load_slab(l, 12 + 2 * hf)
            rb = load_slab(l, 13 + 2 * hf)
            for g in range(NG):
                p = pbn()
                for k in range(8):
                    S.mm(p[:, 0:256], actT[:, k, gc(g)], ra[:, k, :], start=(k == 0), stop=(k == 7))
                for k in range(8):
                    S.mm(p[:, 256:512], actT[:, k, gc(g)], rb[:, k, :], start=(k == 0), stop=(k == 7))
                S.tt("vector", xres[:, g, hf * 512:(hf + 1) * 512], xres[:, g, hf * 512:(hf + 1) * 512], p[:], ALU.add)
                yield
        dump("x1", xres.rearrange("p g d -> p (g d)"), [128, NG * D])
        yield from norm_to_hT(l, C_FFNG, xres, actT)

    def phase_F(l, xb):
        xres = xres2[:, xb]
        actT = actT2[:, xb]
        s = 16
        for sc in range(6):
            nfc = 4 if sc < 5 else 2
            nh = nfc // 2
            for hh in range(nh):
                rg = load_slab(l, s + 2 * hh, F=True)
                ru = load_slab(l, s + 2 * hh + 1, F=True)
                for c2 in range(2):
                    fc = hh * 2 + c2
                    pg = pbF()
                    for k in range(8):
                        S.mm(pg[:], rg[:, k, c2 * 128:(c2 + 1) * 128], actT[:, k, :], start=(k == 0), stop=(k == 7))
                        if k == 3:
                            yield
                    S.act(sg[:, 0, :], pg[:], AF.Silu)
                    yield
                    pu = pbF()
                    for k in range(8):
                        S.mm(pu[:], ru[:, k, c2 * 128:(c2 + 1) * 128], actT[:, k, :], start=(k == 0), stop=(k == 7))
                        if k == 3:
                            yield
                    S.tt("vector", ffT[:, fc, :], pu[:], sg[:, 0, :], ALU.mult)
                    yield
            rd = [load_slab(l, s + 2 * nh + hh, F=True) for hh in range(nh)]
            for g in range(NG):
                for hf in range(2):
                    p = pbF()
                    for fc in range(nfc):
                        S.mm(p[:], ffT[:, fc, gc(g)], rd[fc // 2][:, fc % 2, hf * 512:(hf + 1) * 512],
                             start=(fc == 0), stop=(fc == nfc - 1))
                    S.tt("vector", xres[:, g, hf * 512:(hf + 1) * 512], xres[:, g, hf * 512:(hf + 1) * 512],
                         p[:], ALU.add)
                    yield
            s += 3 * nh
        dump("x2", xres.rearrange("p g d -> p (g d)"), [128, NG * D])

    xv = x_d.rearrange("(t g p) d -> t p g d", g=NG, p=128)
    ov = out_d.rearrange("(t g p) d -> t p g d", g=NG, p=128)

    def phase_X(t):
        S.dma("scalar", xres2[:, t % 2], xv[t], key="xload", ins=[])

    def phase_E(t):
        xres = xres2[:, t % 2]
        S.dma("scalar", fgb, fng_d[:, :], key="fng", ins=[])
        rms_stats(xres)
        for g in range(NG):
            S.stt("vector", xres[:, g, :], xres[:, g, :], rstd[:, g:g + 1], fgb, ALU.mult, ALU.mult)
        S.dma("scalar", ov[t], xres, key="xstore", outs=["out%d" % t])

    S.marks = []

    def mark(name):
        S.marks.append((name, len(S.by_eng["tensor"])))

    def run(gen):
        for _ in gen:
            pass

    def merge(ga, gb, na, nb):
        alive_b = True
        for w in ga:
            for _ in range(1 if w is None else w):
                if alive_b:
                    try:
                        next(gb)
                    except StopIteration:
                        alive_b = False
        if alive_b:
            for _ in gb:
                pass

    jobs = []
    for t0 in range(0, n_tiles, 2):
        ts_ = [t for t in (t0, t0 + 1) if t < n_tiles]
        for l in range(n_layers):
            for t in ts_:
                jobs.append((t, l))
    if not pipeline or n_layers == 0:
        while pre_rest:
            prepass(*pre_rest.pop(0))
        for t in range(n_tiles):
            phase_X(t)
            for l in range(n_layers):
                run(phase_P(l, t % 2))
                if stop == "inproj":
                    continue
                run(phase_M(l, t % 2))
                if stop == "C":
                    continue
                run(phase_Q(l, t % 2))
                run(phase_F(l, t % 2))
            phase_E(t)
    else:
        pending_F = None

        def chain(*gens):
            for g_ in gens:
                yield from g_

        def marked(name, gen):
            mark(name)
            yield from gen
            if pre_rest:
                prepass(*pre_rest.pop(0))

        for (t, l) in jobs:
            if l == 0:
                phase_X(t)
            pm_gen = chain(marked("P%d_%d" % (t, l), phase_P(l, t % 2)), marked("M%d_%d" % (t, l), phase_M(l, t % 2)))
            if pending_F is not None:
                pt, pl = pending_F
                merge(pm_gen, phase_F(pl, pt % 2), MR[0], MR[1])
                if pl == n_layers - 1:
                    phase_E(pt)
            else:
                run(pm_gen)
            mark("Q%d_%d" % (t, l))
            run(phase_Q(l, t % 2))
            if pre_rest:
                prepass(*pre_rest.pop(0))
            pending_F = (t, l)
        while pre_rest:
            prepass(*pre_rest.pop(0))
        mark("Flast")
        pt, pl = pending_F
        run(phase_F(pl, pt % 2))
        phase_E(pt)
        mark("end")

    S.add("sync", lambda e: None, outs=[], ins=["out%d" % t for t in range(n_tiles)] + ["dbg_" + n for n in dump_d])
    stack = contextlib.ExitStack()
    S.finalize(stack, group_keys=("c", "cw") + PREK)
    stack.close()
    return nc, S


def _consts():
    c = np.zeros((128, NCST), np.float32)
    idx = np.arange(128)
    r, q = idx[:, None], idx[None, :]
    c[:, K_ID:K_ID + 128] = (r == q)
    c[:, K_MINCL:K_MINCL + 128] = (r <= q)
    c[:, K_MNEG:K_MNEG + 128] = np.where(q >= r, 0.0, -30000.0)
    c[:, K_MSTR:K_MSTR + 128] = (r < q)
    c[:, K_LEV1:K_LEV1 + 128] = (r < q) & (r // 2 == q // 2)
    c[:, K_ONES:K_ONES + 128] = 1.0
    for lev in range(6):
        b = 2 << lev
        c[:, K_LEVT + lev * 128:K_LEVT + (lev + 1) * 128] = (q < r) & (r // (2 * b) == q // (2 * b)) & (r // b != q // b)
    return c


def _pack(inp):
    f = lambda a: np.asarray(a, np.float32)
    prm = np.zeros((128, L, NP), np.float32)
    wsT = np.zeros((128, L, 512), np.float32)
    for l in range(L):
        prm[:, l, C_MIXG:C_MIXG + 8] = f(inp["mix_norm_g"])[l].reshape(8, 128).T
        prm[:, l, C_FFNG:C_FFNG + 8] = f(inp["ffn_norm_g"])[l].reshape(8, 128).T
        prm[:, l, C_CVW:C_CVW + 62] = f(inp["cv_dw_w"])[l].reshape(31, 2, 128).transpose(2, 1, 0).reshape(128, 62)
        prm[:, l, C_CVB:C_CVB + 2] = f(inp["cv_dw_b"])[l].reshape(2, 128).T
        prm[:, l, C_CVLG:C_CVLG + 2] = f(inp["cv_ln_g"])[l].reshape(2, 128).T
        prm[:, l, C_CVLB:C_CVLB + 2] = f(inp["cv_ln_b"])[l].reshape(2, 128).T
        prm[:, l, C_DNW:C_DNW + 48] = f(inp["dn_conv_w"])[l].reshape(4, 12, 128).transpose(2, 1, 0).reshape(128, 48)
        prm[:, l, C_DNG] = f(inp["dn_norm_g"])[l]
        prm[:, l, C_ALOG:C_ALOG + 4] = f(inp["dn_a_log"])[l][None, :]
        prm[:, l, C_DTB:C_DTB + 4] = f(inp["dn_dt_bias"])[l][None, :]
        prm[:, l, C_BS:C_BS + 256] = np.repeat(f(inp["gm_bs"])[l].T, 64, axis=1)
        prm[:, l, C_GLG:C_GLG + 256] = f(inp["gm_ln_g"])[l][None, :]
        prm[:, l, C_GLB:C_GLB + 256] = f(inp["gm_ln_b"])[l][None, :]
        wsT[:, l, :] = f(inp["gm_ws"])[l].transpose(2, 0, 1).reshape(128, 512)
    fng = np.ascontiguousarray(np.broadcast_to(f(inp["final_norm_g"])[None, :], (128, D)))
    return prm, wsT, fng


_CACHE = {}


def kernel(**inp):
    x = np.ascontiguousarray(np.asarray(inp["x"], np.float32))
    B = x.shape[0]
    if "nc" not in _CACHE:
        _CACHE["nc"] = build(n_tiles=x.shape[1] // TT, n_layers=L)[0]
    nc = _CACHE["nc"]
    prm, wsT, fng = _pack(inp)
    shared = {
        "w_in": np.ascontiguousarray(np.asarray(inp["w_in"], np.float32)),
        "w_out": np.ascontiguousarray(np.asarray(inp["w_out"], np.float32)),
        "w_gate": np.ascontiguousarray(np.asarray(inp["w_gate"], np.float32)),
        "w_up": np.ascontiguousarray(np.asarray(inp["w_up"], np.float32)),
        "w_down": np.ascontiguousarray(np.asarray(inp["w_down"], np.float32)),
        "params": prm, "wsT": wsT, "fng": fng, "consts": _consts(),
    }
    in_maps = [dict(shared, x=x[b]) for b in range(B)]
    res = run_bass_kernel_spmd(nc, in_maps, core_ids=list(range(B)))
    return np.stack([np.asarray(r["out"], np.float32) for r in res.results], axis=0)
```
